# Optimizing a Trainium2 kernel written in Bass

```python
import math, functools
import jax, jax.numpy as jnp
from jax import lax
import numpy as np

D_MODEL = 1024
BATCH = 8
SEQ = 2048
DEPTH = 2
DEC_BATCH = 32
DEC_SEQ = 64
PAST_LEN = 2048

CHUNK = 64
N_A_LAYERS = DEPTH // 2
N_B_LAYERS = DEPTH - N_A_LAYERS
SSM_GROUP = 16
N_GROUPS = D_MODEL // SSM_GROUP
SSM_STATE = 64
HEAD_DIM = 64
N_HEADS = D_MODEL // HEAD_DIM
D_FF = 4 * D_MODEL
Q_BLOCK = 128
EPS = 1e-6
DT_MIN = 0.001
DT_MAX = 0.1
FORGET_BIAS = 5.0
ATTN_SCALE = 1.0 / math.sqrt(HEAD_DIM)

kernel_name = "s5_fox_yoco_macaron_stream_step"


def _rmsnorm(x, g):
    xf = x.astype(jnp.float32)
    xf = xf * lax.rsqrt(jnp.mean(xf * xf, axis=-1, keepdims=True) + EPS)
    return (xf * g.astype(jnp.float32)).astype(x.dtype)


def _swiglu(x, g, wg, wu, wd):
    h = _rmsnorm(x, g)
    return (jax.nn.silu(h @ wg) * (h @ wu)) @ wd


def _ssm_combine(e1, e2):
    a1, b1 = e1
    a2, b2 = e2
    return a1 * a2, a2 * b1 + b2


def _s5(u, h0_re, h0_im, a_re, a_im, log_dt, b_re, b_im, c_re, c_im, d_skip, w_glu_a, w_glu_b):
    f32 = jnp.float32
    bsz, L, _ = u.shape
    uf = u.astype(f32).reshape(bsz, L, N_GROUPS, SSM_GROUP)
    lam = lax.complex(a_re.astype(f32), a_im.astype(f32))
    dt = jnp.exp(log_dt.astype(f32))[:, None]
    abar = jnp.exp(lam * dt)
    bbar = ((abar - 1.0) / lam)[:, :, None] * lax.complex(b_re.astype(f32), b_im.astype(f32))
    cmat = lax.complex(c_re.astype(f32), c_im.astype(f32))
    bu = jnp.einsum("blgc,gpc->blgp", uf.astype(jnp.complex64), bbar)
    h0 = lax.complex(h0_re.astype(f32), h0_im.astype(f32))
    bu = bu.at[:, 0].add(abar * h0)
    a_el = jnp.broadcast_to(abar, (1, L) + abar.shape)
    _, h = lax.associative_scan(_ssm_combine, (a_el, bu), axis=1)
    y = jnp.real(jnp.einsum("blgp,gcp->blgc", h, cmat)) + d_skip.astype(f32).reshape(N_GROUPS, SSM_GROUP) * uf
    z = jax.nn.gelu(y.reshape(bsz, L, D_MODEL)).astype(u.dtype)
    out = (z @ w_glu_a) * jax.nn.sigmoid(z @ w_glu_b)
    h_last = h[:, -1]
    return out, jnp.real(h_last), jnp.imag(h_last)


def _shared_kv(h, kv_norm, w_kvf, b_f, k_norm):
    bsz, L, _ = h.shape
    p = _rmsnorm(h, kv_norm) @ w_kvf
    k = _rmsnorm(p[..., :D_MODEL].reshape(bsz, L, N_HEADS, HEAD_DIM), k_norm)
    v = p[..., D_MODEL:2 * D_MODEL].reshape(bsz, L, N_HEADS, HEAD_DIM)
    logf = jax.nn.log_sigmoid((p[..., 2 * D_MODEL:] + b_f).astype(jnp.float32))
    return k, v, logf


def _fox_block(q, k, v, cqT, ckT, qpos, kpos):
    s = jnp.einsum("bqhd,bkhd->bhqk", q, k).astype(jnp.float32) * ATTN_SCALE
    s = s + cqT[..., :, None] - ckT[..., None, :]
    s = jnp.where(kpos[None, :] <= qpos[:, None], s, -jnp.inf)
    p = jax.nn.softmax(s, axis=-1)
    return jnp.einsum("bhqk,bkhd->bqhd", p.astype(v.dtype), v)


def _fox_prompt(q, k, v, logf):
    bsz, S = q.shape[0], q.shape[1]
    cT = jnp.swapaxes(jnp.cumsum(logf, axis=1), 1, 2)
    kpos = jnp.arange(S)

    def one(i):
        start = i * Q_BLOCK
        qb = lax.dynamic_slice_in_dim(q, start, Q_BLOCK, axis=1)
        cqb = lax.dynamic_slice_in_dim(cT, start, Q_BLOCK, axis=2)
        return _fox_block(qb, k, v, cqb, cT, start + jnp.arange(Q_BLOCK), kpos)

    o = lax.map(one, jnp.arange(S // Q_BLOCK))
    return jnp.moveaxis(o, 0, 1).reshape(bsz, S, N_HEADS, HEAD_DIM)


def _fox_sample(q, k, v, logf, cache_k, cache_v, cache_logf):
    P, Lq = cache_k.shape[1], q.shape[1]
    c_past = jnp.cumsum(cache_logf.astype(jnp.float32), axis=1)
    c_new = c_past[:, -1:] + jnp.cumsum(logf, axis=1)
    k_all = jnp.concatenate([cache_k, k], axis=1)
    v_all = jnp.concatenate([cache_v, v], axis=1)
    c_allT = jnp.swapaxes(jnp.concatenate([c_past, c_new], axis=1), 1, 2)
    return _fox_block(q, k_all, v_all, jnp.swapaxes(c_new, 1, 2), c_allT, P + jnp.arange(Lq), jnp.arange(P + Lq))


def _trunk(x, h0_re, h0_im, attend, w):
    bsz, L, _ = x.shape
    ssm_re, ssm_im = [], []
    kv = None
    for l in range(DEPTH):
        x = x + 0.5 * _swiglu(x, w["ffn_norm"][l, 0], w["w_ffn_gate"][l, 0], w["w_ffn_up"][l, 0], w["w_ffn_down"][l, 0])
        u = _rmsnorm(x, w["mix_norm"][l])
        if l < N_A_LAYERS:
            y, hr, hi = _s5(u, h0_re[:, l], h0_im[:, l], w["ssm_a_re"][l], w["ssm_a_im"][l], w["ssm_log_dt"][l],
                            w["ssm_b_re"][l], w["ssm_b_im"][l], w["ssm_c_re"][l], w["ssm_c_im"][l],
                            w["ssm_d"][l], w["w_glu_a"][l], w["w_glu_b"][l])
            ssm_re.append(hr)
            ssm_im.append(hi)
        else:
            j = l - N_A_LAYERS
            q = _rmsnorm((u @ w["w_q"][j]).reshape(bsz, L, N_HEADS, HEAD_DIM), w["q_norm"][j])
            y = attend(q, kv[0], kv[1], kv[2]).reshape(bsz, L, D_MODEL) @ w["w_o"][j]
        x = x + y.astype(x.dtype)
        x = x + 0.5 * _swiglu(x, w["ffn_norm"][l, 1], w["w_ffn_gate"][l, 1], w["w_ffn_up"][l, 1], w["w_ffn_down"][l, 1])
        if l == N_A_LAYERS - 1:
            kv = _shared_kv(x, w["kv_norm"], w["w_kvf"], w["b_f"], w["k_norm"])
    return x, jnp.stack(ssm_re, axis=1), jnp.stack(ssm_im, axis=1), kv


def setup_inputs(seed: int = 0) -> dict:
    key = jax.random.key(seed)
    ks = jax.random.split(key, 32)
    f32 = jnp.float32

    def nrm(k, shape, scale=1.0):
        return scale * jax.random.normal(k, shape, f32)

    dsc = D_MODEL ** -0.5
    return {
        "x_prompt": nrm(ks[0], (BATCH, SEQ, D_MODEL)),
        "x_sample": nrm(ks[1], (DEC_BATCH, DEC_SEQ, D_MODEL)),
        "cache_k": nrm(ks[2], (DEC_BATCH, PAST_LEN, N_HEADS, HEAD_DIM)),
        "cache_v": nrm(ks[3], (DEC_BATCH, PAST_LEN, N_HEADS, HEAD_DIM)),
        "cache_logf": jax.nn.log_sigmoid(FORGET_BIAS + nrm(ks[4], (DEC_BATCH, PAST_LEN, N_HEADS), 0.5)),
        "state_ssm_re": nrm(ks[5], (DEC_BATCH, N_A_LAYERS, N_GROUPS, SSM_STATE), 0.1),
        "state_ssm_im": nrm(ks[6], (DEC_BATCH, N_A_LAYERS, N_GROUPS, SSM_STATE), 0.1),
        "ffn_norm": 1.0 + nrm(ks[7], (DEPTH, 2, D_MODEL), 0.01),
        "w_ffn_gate": nrm(ks[8], (DEPTH, 2, D_MODEL, D_FF), dsc),
        "w_ffn_up": nrm(ks[9], (DEPTH, 2, D_MODEL, D_FF), dsc),
        "w_ffn_down": nrm(ks[10], (DEPTH, 2, D_FF, D_MODEL), D_FF ** -0.5),
        "mix_norm": 1.0 + nrm(ks[11], (DEPTH, D_MODEL), 0.01),
        "ssm_a_re": -0.5 + nrm(ks[12], (N_A_LAYERS, N_GROUPS, SSM_STATE), 0.01),
        "ssm_a_im": jnp.pi * jnp.arange(SSM_STATE, dtype=f32) + nrm(ks[13], (N_A_LAYERS, N_GROUPS, SSM_STATE), 0.01),
        "ssm_log_dt": jax.random.uniform(ks[14], (N_A_LAYERS, N_GROUPS), f32, math.log(DT_MIN), math.log(DT_MAX)),
        "ssm_b_re": nrm(ks[15], (N_A_LAYERS, N_GROUPS, SSM_STATE, SSM_GROUP), (2 * SSM_GROUP) ** -0.5),
        "ssm_b_im": nrm(ks[16], (N_A_LAYERS, N_GROUPS, SSM_STATE, SSM_GROUP), (2 * SSM_GROUP) ** -0.5),
        "ssm_c_re": nrm(ks[17], (N_A_LAYERS, N_GROUPS, SSM_GROUP, SSM_STATE), SSM_STATE ** -0.5),
        "ssm_c_im": nrm(ks[18], (N_A_LAYERS, N_GROUPS, SSM_GROUP, SSM_STATE), SSM_STATE ** -0.5),
        "ssm_d": nrm(ks[19], (N_A_LAYERS, D_MODEL)),
        "w_glu_a": nrm(ks[20], (N_A_LAYERS, D_MODEL, D_MODEL), dsc),
        "w_glu_b": nrm(ks[21], (N_A_LAYERS, D_MODEL, D_MODEL), dsc),
        "kv_norm": 1.0 + nrm(ks[22], (D_MODEL,), 0.01),
        "w_kvf": jnp.concatenate([nrm(ks[23], (D_MODEL, 2 * D_MODEL), dsc),
                                  nrm(ks[24], (D_MODEL, N_HEADS), 0.1 * dsc)], axis=1),
        "b_f": FORGET_BIAS + nrm(ks[25], (N_HEADS,), 0.5),
        "k_norm": 1.0 + nrm(ks[26], (HEAD_DIM,), 0.01),
        "w_q": nrm(ks[27], (N_B_LAYERS, D_MODEL, D_MODEL), dsc),
        "q_norm": 1.0 + nrm(ks[28], (N_B_LAYERS, HEAD_DIM), 0.01),
        "w_o": nrm(ks[29], (N_B_LAYERS, D_MODEL, D_MODEL), dsc),
    }


def reference(x_prompt, x_sample, cache_k, cache_v, cache_logf, state_ssm_re, state_ssm_im,
              ffn_norm, w_ffn_gate, w_ffn_up, w_ffn_down, mix_norm,
              ssm_a_re, ssm_a_im, ssm_log_dt, ssm_b_re, ssm_b_im, ssm_c_re, ssm_c_im, ssm_d,
              w_glu_a, w_glu_b, kv_norm, w_kvf, b_f, k_norm, w_q, q_norm, w_o):
    w = dict(ffn_norm=ffn_norm, w_ffn_gate=w_ffn_gate, w_ffn_up=w_ffn_up, w_ffn_down=w_ffn_down,
             mix_norm=mix_norm, ssm_a_re=ssm_a_re, ssm_a_im=ssm_a_im, ssm_log_dt=ssm_log_dt,
             ssm_b_re=ssm_b_re, ssm_b_im=ssm_b_im, ssm_c_re=ssm_c_re, ssm_c_im=ssm_c_im, ssm_d=ssm_d,
             w_glu_a=w_glu_a, w_glu_b=w_glu_b, kv_norm=kv_norm, w_kvf=w_kvf, b_f=b_f, k_norm=k_norm,
             w_q=w_q, q_norm=q_norm, w_o=w_o)
    h0 = jnp.zeros((x_prompt.shape[0], N_A_LAYERS, N_GROUPS, SSM_STATE), jnp.float32)
    y_prompt, p_ssm_re, p_ssm_im, p_kv = _trunk(x_prompt, h0, h0, _fox_prompt, w)
    attend_sample = functools.partial(_fox_sample, cache_k=cache_k, cache_v=cache_v, cache_logf=cache_logf)
    y_sample, s_ssm_re, s_ssm_im, s_kv = _trunk(x_sample, state_ssm_re, state_ssm_im, attend_sample, w)
    return (y_prompt, y_sample, p_ssm_re, p_ssm_im, p_kv[0], p_kv[1], p_kv[2],
            s_ssm_re, s_ssm_im, s_kv[0], s_kv[1], s_kv[2])
```

```python
import math
import numpy as np
from contextlib import ExitStack
import concourse.bass as bass
import concourse.mybir as mybir
from concourse.bass_utils import run_bass_kernel_spmd

F32 = mybir.dt.float32
BF16 = mybir.dt.bfloat16
ALU = mybir.AluOpType
AF = mybir.ActivationFunctionType
AX = mybir.AxisListType

D = 1024
KC = 8
DFF = 4096
FG = 256
NGRP = DFF // FG
NT = 18
NTOK = NT * 128
NPT = 16
S = 2048
NH = 16
HD = 64
NCH = NTOK // 8
EPS = 1e-6
SCALE = 1.0 / 8.0
PI = math.pi


class Buf:
    __slots__ = ("name", "w", "r")

    def __init__(self, name=""):
        self.name = name
        self.w = None
        self.r = []


class Stream:
    def __init__(self, name, handle, sem):
        self.name = name
        self.h = handle
        self.sem = sem
        self.items = []
        self.n = 0
        self.waited = set()
        self.known = {}


class Prog:
    def __init__(self, nc, es):
        self.nc = nc
        self.es = es
        self.streams = {}
        for nm in ("tensor", "vector", "scalar", "gpsimd", "sync"):
            sem = es.enter_context(nc.semaphore("sem_" + nm))
            self.streams[nm] = Stream(nm, getattr(nc, nm), sem)
        self.dma_sems = {}
        self.nbank = 0

    def dma_sem(self, name):
        if name not in self.dma_sems:
            self.dma_sems[name] = [self.es.enter_context(self.nc.semaphore("dsem_" + name)), 0]
        return self.dma_sems[name]

    def _wait(self, st, sp, raw):
        if sp is None:
            return
        if sp[0] == 'c':
            _, s2, idx = sp
            if s2 is st and not raw and st.name == "tensor":
                return
            key = s2.name
            if st.known.get(key, -1) >= idx:
                return
            st.known[key] = idx
            s2.waited.add(idx)
            st.items.append(('wc', s2, idx))
        else:
            _, sem, val = sp
            key = id(sem)
            if st.known.get(key, -1) >= val:
                return
            st.known[key] = val
            st.items.append(('wd', sem, val))

    def emit(self, eng, fn, reads=(), writes=(), dma=None):
        st = self.streams[eng]
        for b in reads:
            self._wait(st, b.w, True)
        for b in writes:
            self._wait(st, b.w, False)
            for sp in b.r:
                self._wait(st, sp, False)
        idx = st.n
        st.n += 1
        if dma is not None:
            d = self.dma_sem(dma)
            d[1] += 16
            sp = ('d', d[0], d[1])
            st.items.append(('dma', fn, d[0]))
        else:
            sp = ('c', st, idx)
            st.items.append(('ins', fn, idx))
        for b in reads:
            b.r.append(sp)
            if len(b.r) > 16:
                last = {}
                for q in b.r:
                    k = q[1].name if q[0] == 'c' else id(q[1])
                    if k not in last or last[k][2] < q[2]:
                        last[k] = q
                b.r = list(last.values())
        for b in writes:
            b.w = sp
            b.r = []
        return sp

    def barrier(self):
        sps = []
        for st in self.streams.values():
            if st.name != "sync" and st.n > 0:
                for it in reversed(st.items):
                    if it[0] == 'ins':
                        sps.append(('c', st, it[2]))
                        break
        for sem, val in self.dma_sems.values():
            if val > 0:
                sps.append(('d', sem, val))
        for st in self.streams.values():
            for sp in sps:
                self._wait(st, sp, True)

    def finalize(self):
        nc = self.nc
        self.barrier()
        ranks = {}
        for st in self.streams.values():
            ranks[st.name] = {idx: k + 1 for k, idx in enumerate(sorted(st.waited))}
        with nc.Block() as block:
            def mk(st):
                def body(h):
                    for it in st.items:
                        if it[0] == 'wc':
                            h.wait_ge(it[1].sem, ranks[it[1].name][it[2]])
                        elif it[0] == 'wd':
                            h.wait_ge(it[1], it[2])
                        elif it[0] == 'dma':
                            it[1](h).then_inc(it[2], 16)
                        else:
                            ins = it[1](h)
                            if it[2] in st.waited:
                                ins.then_inc(st.sem, 1)
                return body
            for nm, st in self.streams.items():
                if st.items:
                    getattr(block, nm)(mk(st))


class _Stop(Exception):
    pass


def build_program(upto=99):
    nc = bass.Bass("TRN2", target_bir_lowering=False)
    es = ExitStack()
    with es:
        es.enter_context(nc.allow_non_contiguous_dma(reason="small strided parameter/state loads"))
        es.enter_context(nc.allow_low_precision(reason="bf16 matmul operands, fp32 accumulation"))
        P = Prog(nc, es)

        def din(name, shape):
            return nc.dram_tensor(name, list(shape), F32, kind="ExternalInput").ap()

        def dout(name, shape, dt=F32):
            return nc.dram_tensor(name, list(shape), dt, kind="ExternalOutput").ap()

        xp = din("xp", [S, D]); xs = din("xs", [256, D])
        ck_d = din("ck", [4, S, D]); cv_d = din("cv", [4, S, D]); clf_d = din("clf", [4, S, NH])
        h0r_d = din("h0r", [4, 64, 64]); h0i_d = din("h0i", [4, 64, 64])
        ffn_norm = din("ffn_norm", [4, D])
        wg_d = din("wg", [4, D, DFF]); wu_d = din("wu", [4, D, DFF]); wd_d = din("wd", [4, DFF, D])
        mix_norm = din("mix_norm", [2, D])
        a_re = din("a_re", [64, 64]); a_im = din("a_im", [64, 64]); log_dt = din("log_dt", [64])
        b_re = din("b_re", [64, 64, 16]); b_im = din("b_im", [64, 64, 16])
        c_re = din("c_re", [64, 16, 64]); c_im = din("c_im", [64, 16, 64])
        ssm_d = din("ssm_d", [D])
        w_glu_a = din("w_glu_a", [D, D]); w_glu_b = din("w_glu_b", [D, D])
        kv_norm = din("kv_norm", [D]); w_kvf = din("w_kvf", [D, 2064]); b_f = din("b_f", [NH])
        k_norm = din("k_norm", [HD]); w_q = din("w_q", [D, D]); q_norm = din("q_norm", [HD]); w_o = din("w_o", [D, D])

        yp = dout("yp", [S, D]); ys = dout("ys", [256, D])
        psr = dout("psr", [64, 64]); psi = dout("psi", [64, 64])
        pk = dout("pk", [S, D]); pv = dout("pv", [S, D]); plf = dout("plf", [S, NH])
        ssr = dout("ssr", [4, 64, 64]); ssi = dout("ssi", [4, 64, 64])
        sk = dout("sk", [256, D]); sv = dout("sv", [256, D]); slf = dout("slf", [256, NH])
        tbl_d = dout("scr_tbl", [8, 128, 3072], BF16)
        etb_d = dout("scr_etb", [8, 128, 3, 4 * NCH])

        def sb(name, shape, dt):
            return es.enter_context(nc.sbuf_tensor(name, list(shape), dt))

        X = sb("X", [128, NT, D], F32)
        bX = [Buf(f"X{t}") for t in range(NT)]
        hT = sb("hT", [128, KC, NTOK], BF16)
        bhT = [Buf(f"hT{t}") for t in range(NT)]
        ident = sb("ident", [128, 128], BF16); identf = sb("identf", [128, 128], F32)
        bconst = Buf("const")
        Wm = sb("Wm", [128, 8, 240], BF16)
        trif = sb("trif", [128, 128], F32)
        tri2f = sb("tri2f", [128, 128], F32)
        onesf = sb("onesf", [128, 128], F32)
        onesA = sb("onesA", [128, 128], F32)
        onesB = sb("onesB", [128, 128], F32)
        sel64 = sb("sel64", [128, 128], F32)
        sel32 = sb("sel32", [128, 128], F32)
        sel96 = sb("sel96", [128, 128], F32)
        trib = sb("trib", [128, 128], BF16)
        mask64 = sb("mask64", [128, 64], BF16)
        rstd = sb("rstd", [128, NT], F32); brstd = Buf("rstd")
        ssq = sb("ssq", [128, NT], F32); bssq = Buf("ssq")
        LF = sb("LF", [128, NT, NH], F32); bLF = Buf("LF")
        maskT = sb("maskT", [128, NH * 16], F32)
        scn = sb("scn", [128, NH * 16], F32); bscn = Buf("scn")
        CKc = sb("CKc", [128, NT, NH], F32); bCK = Buf("CK")
        Dfm = sb("Dfm", [128, KC], F32)
        bfb = sb("bfb", [128, NH], F32)
        knb = sb("knb", [128, HD], F32); qnb = sb("qnb", [128, HD], F32)
        Hfin = sb("Hfin", [128, 2, 5, 32], F32); bHfin = Buf("Hfin")
        A8s = sb("A8s", [128, 2, 8, 32], F32)
        ARENA_F32 = 20736
        arena = sb("arena", [128, ARENA_F32], F32)
        ps = [es.enter_context(nc.psum_tensor(f"ps{i}", [128, 512], F32)) for i in range(8)]
        bps = [Buf(f"ps{i}") for i in range(8)]

        pool = {"ids": list(range(8)), "acc": 0, "s": 0}

        def bank():
            ids = pool["ids"]
            i = ids[P.nbank % len(ids)]
            P.nbank += 1
            return ps[i], bps[i]

        def sbank():
            i = pool["s"] % 4
            pool["s"] += 1
            return ps[i], bps[i]

        def acc_bank():
            i = 6 + pool["acc"] % 2
            pool["acc"] += 1
            return ps[i], bps[i]

        class Carver:
            def __init__(self):
                self.off = 0

            def take(self, shape, dt):
                n = int(np.prod(shape[1:]))
                words = n if dt == F32 else (n + 1) // 2
                words = (words + 7) // 8 * 8
                assert self.off + words <= ARENA_F32, (self.off, words)
                v = arena[:, self.off:self.off + words]
                self.off += words
                if dt != F32:
                    v = v.bitcast(dt)[:, 0:n]
                else:
                    v = v[:, 0:n]
                if len(shape) == 2:
                    return v
                names = " ".join(f"d{i}" for i in range(len(shape) - 1))
                kw = {f"d{i}": shape[i + 1] for i in range(len(shape) - 1)}
                return v.rearrange(f"p ({names}) -> p {names}", **kw)

        def E(eng, fn, R=(), W=()):
            return P.emit(eng, fn, R, W)

        def MM(out, lhsT, rhs, start, stop, R, W, **kw):
            return P.emit("tensor", lambda h: h.matmul(out, lhsT=lhsT, rhs=rhs, start=start, stop=stop, **kw), R, W)

        def TR(out, in_, idt, R, W):
            return P.emit("tensor", lambda h: h.transpose(out=out, in_=in_, identity=idt), R, W)

        def ACT(out, in_, func, R, W, **kw):
            return P.emit("scalar", lambda h: h.activation(out=out, in_=in_, func=func, **kw), R, W)

        def TT(eng, out, in0, in1, op, R, W):
            return P.emit(eng, lambda h: h.tensor_tensor(out=out, in0=in0, in1=in1, op=op), R, W)

        def TS(eng, out, in0, s1, s2, op0, op1, R, W):
            if op1 is None:
                return P.emit(eng, lambda h: h.tensor_scalar(out=out, in0=in0, scalar1=s1, scalar2=None, op0=op0), R, W)
            return P.emit(eng, lambda h: h.tensor_scalar(out=out, in0=in0, scalar1=s1, scalar2=s2, op0=op0, op1=op1), R, W)

        def STT(out, in0, scalar, in1, op0, op1, R, W):
            return P.emit("vector", lambda h: h.scalar_tensor_tensor(out=out, in0=in0, scalar=scalar, in1=in1, op0=op0, op1=op1), R, W)

        def CP(eng, out, in_, R, W):
            if eng == "scalar":
                return P.emit(eng, lambda h: h.copy(out=out, in_=in_), R, W)
            return P.emit(eng, lambda h: h.tensor_copy(out=out, in_=in_), R, W)

        def MS(eng, ap, val, W):
            return P.emit(eng, lambda h: h.memset(ap, val), (), W)

        udma = [0]
        NU = 16

        def DMAU(out, in_, R, W, eng="sync"):
            k = udma[0] % NU
            udma[0] += 1
            name = f"u{k}"
            d = P.dma_sem(name)
            st = P.streams[eng]
            if d[1] > 0:
                P._wait(st, ('d', d[0], d[1]), True)
            return P.emit(eng, lambda h: h.dma_start(out=out, in_=in_), R, W, dma=name)

        def DMA(out, in_, R, W, eng="sync", sem=None):
            return DMAU(out, in_, R, W, eng=eng)

        def affine(out, in_, pattern, cmp, fill, base, cm, R, W):
            return P.emit("gpsimd", lambda h: h.affine_select(out=out, in_=in_, pattern=pattern, compare_op=cmp,
                                                              fill=fill, base=base, channel_multiplier=cm), R, W)

        MS("gpsimd", identf[:], 1.0, [bconst])
        affine(identf[:], identf[:], [[1, 128]], ALU.is_equal, 0.0, 0, -1, [bconst], [bconst])
        CP("gpsimd", ident[:], identf[:], [bconst], [bconst])
        MS("gpsimd", onesf[:], 1.0, [bconst])
        affine(trif[:], onesf[:], [[1, 128]], ALU.is_ge, 0.0, 0, -1, [bconst], [bconst])
        CP("gpsimd", tri2f[:], trif[:], [bconst], [bconst])
        MS("gpsimd", tri2f[0:64, 64:128], 0.0, [bconst])
        MS("gpsimd", onesA[:], 0.0, [bconst]); MS("gpsimd", onesA[:, 0:64], 1.0, [bconst])
        MS("gpsimd", onesB[:], 0.0, [bconst]); MS("gpsimd", onesB[:, 64:128], 1.0, [bconst])
        affine(sel64[:], onesf[:], [[0, 128]], ALU.is_equal, 0.0, -64, 1, [bconst], [bconst])
        affine(sel32[:], onesf[:], [[0, 128]], ALU.is_equal, 0.0, -32, 1, [bconst], [bconst])
        affine(sel96[:], onesf[:], [[0, 128]], ALU.is_equal, 0.0, -96, 1, [bconst], [bconst])
        CP("gpsimd", trib[:], trif[:], [bconst], [bconst])
        MS("gpsimd", Wm[:], 0.0, [bconst])
        for q in range(8):
            CP("gpsimd", Wm[:, q, 112:128], identf[:, 16 * q:16 * q + 16], [bconst], [bconst])
        MS("gpsimd", maskT[:], 1.0, [bconst])
        MS("gpsimd", maskT[:].rearrange("p (h t) -> p h t", t=16)[:, :, 0:1], 0.0, [bconst])
        MS("gpsimd", mask64[:], 1.0, [bconst])
        affine(mask64[0:64, :], mask64[0:64, :], [[1, 64]], ALU.is_ge, 0.0, 0, -1, [bconst], [bconst])
        affine(mask64[64:128, :], mask64[64:128, :], [[1, 64]], ALU.is_ge, 0.0, 0, -1, [bconst], [bconst])
        DMA(Dfm[:], ssm_d.rearrange("(k p) -> p k", p=128), [], [bconst], sem="c0")
        DMA(bfb[:], b_f.partition_broadcast(128), [], [bconst], sem="c0")
        DMA(knb[:], k_norm.partition_broadcast(128), [], [bconst], sem="c0")
        DMA(qnb[:], q_norm.partition_broadcast(128), [], [bconst], sem="c0")
        for t in range(NPT):
            DMAU(X[:, t, :], xp[t * 128:(t + 1) * 128, :], [], [bX[t]])
        for t in range(2):
            DMAU(X[:, NPT + t, :], xs[t * 128:(t + 1) * 128, :], [], [bX[NPT + t]])

        def phase0():
            cv = Carver()
            b0 = Buf("p0")
            araw = cv.take([128, 2, 128], F32)
            A = cv.take([128, 2, 64], F32)
            dtb = cv.take([128, 64], F32)
            Pw = cv.take([128, 9, 2, 64], F32)
            t1 = cv.take([128, 64], F32); t2 = cv.take([128, 64], F32); t3 = cv.take([128, 64], F32)
            qq = cv.take([128, 2, 64], F32)
            A8 = cv.take([128, 8, 2, 64], F32)
            for ri, src in enumerate((a_re, a_im)):
                DMA(araw[0:64, ri, 0:64], src, [], [b0], sem="c0")
                DMA(araw[0:64, ri, 64:128], src, [], [b0], sem="c0")
            DMA(dtb[:], log_dt.partition_broadcast(128), [], [b0], sem="c0")
            for ri in range(2):
                pb, bb = bank()
                TR(pb[:, 0:64], araw[0:64, ri, :], identf[0:64, 0:64], [b0, bconst], [bb])
                CP("vector", A[:, ri, :], pb[:, 0:64], [bb], [b0])
            ACT(dtb, dtb, AF.Exp, [b0], [b0])
            ar, ai = A[:, 0, :], A[:, 1, :]
            TT("vector", t1, ar, dtb, ALU.mult, [b0], [b0])
            ACT(t1, t1, AF.Exp, [b0], [b0])
            TT("vector", t2, ai, dtb, ALU.mult, [b0], [b0])

            def sin_of(dst, src, shift):
                TS("vector", t3, src, shift, None, ALU.add, None, [b0], [b0])
                CP("vector", dst, t3, [b0], [b0])
                for kk in range(1, 7):
                    thr = (2 * kk - 1) * PI
                    E("vector", lambda h, thr=thr: h.tensor_scalar(out=qq[:, 0, :], in0=t3, scalar1=thr, scalar2=-2 * PI,
                                                                   op0=ALU.is_gt, op1=ALU.mult), [b0], [b0])
                    TT("vector", dst, dst, qq[:, 0, :], ALU.add, [b0], [b0])
                ACT(dst, dst, AF.Sin, [b0], [b0])

            sin_of(Pw[:, 1, 1, :], t2, 0.0)
            sin_of(Pw[:, 1, 0, :], t2, PI / 2)
            TT("vector", Pw[:, 1, 0, :], Pw[:, 1, 0, :], t1, ALU.mult, [b0], [b0])
            TT("vector", Pw[:, 1, 1, :], Pw[:, 1, 1, :], t1, ALU.mult, [b0], [b0])
            MS("vector", Pw[:, 0, 0, :], 1.0, [b0]); MS("vector", Pw[:, 0, 1, :], 0.0, [b0])

            def cmul(dr, di, xr, xi, yr, yi, eng="vector"):
                TT(eng, t1, xr, yr, ALU.mult, [b0], [b0])
                TT(eng, t2, xi, yi, ALU.mult, [b0], [b0])
                TT(eng, t3, xr, yi, ALU.mult, [b0], [b0])
                TT(eng, dr, t1, t2, ALU.subtract, [b0], [b0])
                TT(eng, t1, xi, yr, ALU.mult, [b0], [b0])
                TT(eng, di, t3, t1, ALU.add, [b0], [b0])

            for tau in range(2, 9):
                cmul(Pw[:, tau, 0, :], Pw[:, tau, 1, :], Pw[:, tau - 1, 0, :], Pw[:, tau - 1, 1, :], Pw[:, 1, 0, :], Pw[:, 1, 1, :])
            CP("vector", A8[:, 0, :, :], Pw[:, 8, :, :], [b0], [b0])
            for k in range(1, 8):
                cmul(A8[:, k, 0, :], A8[:, k, 1, :], A8[:, k - 1, 0, :], A8[:, k - 1, 1, :], A8[:, k - 1, 0, :], A8[:, k - 1, 1, :])
            TT("vector", t1, ar, ar, ALU.mult, [b0], [b0])
            TT("vector", t2, ai, ai, ALU.mult, [b0], [b0])
            TT("vector", t1, t1, t2, ALU.add, [b0], [b0])
            E("vector", lambda h: h.reciprocal(out=t1, in_=t1), [b0], [b0])
            TS("vector", t2, Pw[:, 1, 0, :], -1.0, None, ALU.add, None, [b0], [b0])
            TT("vector", t3, t2, ar, ALU.mult, [b0], [b0])
            TT("vector", qq[:, 0, :], Pw[:, 1, 1, :], ai, ALU.mult, [b0], [b0])
            TT("vector", qq[:, 0, :], qq[:, 0, :], t3, ALU.add, [b0], [b0])
            TT("vector", t3, t2, ai, ALU.mult, [b0], [b0])
            TT("vector", qq[:, 1, :], Pw[:, 1, 1, :], ar, ALU.mult, [b0], [b0])
            TT("vector", qq[:, 1, :], qq[:, 1, :], t3, ALU.subtract, [b0], [b0])
            TT("vector", qq[:, 0, :], qq[:, 0, :], t1, ALU.mult, [b0], [b0])
            TT("vector", qq[:, 1, :], qq[:, 1, :], t1, ALU.mult, [b0], [b0])

            for ri in range(2):
                for k in range(8):
                    src = A8[:, k, ri, :].rearrange("p (a b) -> p a b", b=2)
                    CP("vector", A8s[0:64, ri, k, :], src[0:64, :, 0], [b0], [bA8s])
                    CP("vector", A8s[64:128, ri, k, :], src[64:128, :, 1], [b0], [bA8s])

            PwR = cv.take([128, 2, 64, 8], F32)
            for j in range(8):
                for ri in range(2):
                    CP("gpsimd", PwR[:, ri, :, j], Pw[:, 7 - j, ri, :], [b0], [b0])
            Braw = cv.take([128, 2, 8, 16], F32)
            Craw = cv.take([128, 2, 128], F32)
            Cc = cv.take([128, 2, 8, 16], F32)
            bbar = cv.take([128, 2, 8, 16], F32)
            tm = [cv.take([128, 1152], F32) for _ in range(4)]
            CAb = cv.take([128, 2, 9, 8, 16], F32)
            BBst = cv.take([128, 8, 16], F32)
            CAst = cv.take([128, 8, 128], F32)
            Kpad = cv.take([128, 8, 240], BF16)
            Bpw = cv.take([128, 2, 8, 128], F32)
            TB2 = [cv.take([128, 3072], BF16) for _ in range(2)]
            bBr, bCr, bCc_, bbb, bCA, bBB, bCAp, bBpw, bTB = [Buf(n) for n in "Braw Craw Cc bbar CAb BBpad CApad Bpw TB".split()]
            btm = [Buf(f"tm{i}") for i in range(4)]
            bKp = Buf("Kpad")
            MS("vector", Kpad, 0.0, [bKp])

            def cmulv(dr, di, xr, xi, yr, yi, shape, Rx, Wd, neg_im=False):
                n = int(np.prod(shape))
                names = " ".join(f"d{i}" for i in range(len(shape)))
                kw = {f"d{i}": shape[i] for i in range(len(shape))}
                tv = [t_[:, 0:n].rearrange(f"p ({names}) -> p {names}", **kw) for t_ in tm]
                TT("vector", tv[0], xr, yr, ALU.mult, Rx, [btm[0]])
                TT("gpsimd", tv[1], xi, yi, ALU.mult, Rx, [btm[1]])
                TT("vector", tv[2], xr, yi, ALU.mult, Rx, [btm[2]])
                TT("vector", tv[3], xi, yr, ALU.mult, Rx, [btm[3]])
                TT("vector", dr, tv[0], tv[1], ALU.subtract, [btm[0], btm[1]], Wd)
                if neg_im:
                    E("vector", lambda h: h.scalar_tensor_tensor(out=di, in0=tv[2], scalar=-1.0, in1=tv[3], op0=ALU.mult, op1=ALU.subtract),
                      [btm[2], btm[3]], Wd)
                else:
                    TT("vector", di, tv[2], tv[3], ALU.add, [btm[2], btm[3]], Wd)

            bTB2 = [Buf("TB0"), Buf("TB1")]
            for fc in range(8):
                gs = slice(fc * 8, fc * 8 + 8)
                TB = TB2[fc % 2]; bTB = bTB2[fc % 2]
                TBt = TB[:, 0:1024].rearrange("p (g m) -> p g m", g=8)
                TBb = TB[:, 1024:2048].rearrange("p (g r m) -> p g r m", g=8, r=2)
                TBc = TB[:, 2048:3072].rearrange("p (a r m) -> p a r m", a=4, r=2)
                for ri, src in enumerate((b_re, b_im)):
                    for half in range(2):
                        DMA(Braw[half * 64:(half + 1) * 64, ri, :, :], src[gs].rearrange("g p c -> p g c"), [], [bBr], sem="c0")
                for ri, src in enumerate((c_re, c_im)):
                    rows = src[gs].rearrange("g c p -> (g c) p")
                    DMA(Craw[:, ri, 0:64], rows, [], [bCr], sem="c0")
                    DMA(Craw[:, ri, 64:128], rows, [], [bCr], sem="c0")
                for ri in range(2):
                    pb, bb = bank()
                    TR(pb[:, 0:128], Craw[:, ri, :], identf[:], [bCr, bconst], [bb])
                    CP("scalar", Cc[:, ri, :, :], pb[:, 0:128].rearrange("p (g c) -> p g c", g=8), [bb], [bCc_])
                bq = lambda ri: qq[:, ri, gs].unsqueeze(2).to_broadcast([128, 8, 16])
                cmulv(bbar[:, 0], bbar[:, 1], Braw[:, 0], Braw[:, 1], bq(0), bq(1), [8, 16], [bBr, b0], [bbb])
                CP("scalar", BBst[0:64], bbar[0:64, 0], [bbb], [bBB])
                CP("scalar", BBst[64:128], bbar[64:128, 1], [bbb], [bBB])
                cb = lambda ri: Cc[:, ri].unsqueeze(1).to_broadcast([128, 9, 8, 16])
                pbc = lambda ri: Pw[:, :, ri, gs].unsqueeze(3).to_broadcast([128, 9, 8, 16])
                cmulv(CAb[:, 0], CAb[:, 1], cb(0), cb(1), pbc(0), pbc(1), [9, 8, 16], [bCc_, b0], [bCA], neg_im=True)
                for hf, ri in ((0, 0), (1, 1)):
                    rws = slice(64 * hf, 64 * hf + 64)
                    CP("scalar" if hf else "gpsimd", CAst[rws].rearrange("p g (t c) -> p t g c", c=16), CAb[rws, ri, 0:8], [bCA], [bCAp])
                for ri in range(2):
                    v = CAb[:, ri, 1:9].rearrange("p t (a b) c -> p t a b c", b=2)
                    CP("scalar", TBc[0:64, :, ri, :].rearrange("p a (i c) -> p i a c", c=16), v[0:64, :, :, 0, :], [bCA], [bTB])
                    CP("scalar", TBc[64:128, :, ri, :].rearrange("p a (i c) -> p i a c", c=16), v[64:128, :, :, 1, :], [bCA], [bTB])
                bb_ = lambda ri: bbar[:, ri].unsqueeze(2).to_broadcast([128, 8, 8, 16])
                pr_ = lambda ri: PwR[:, ri, gs, :].unsqueeze(3).to_broadcast([128, 8, 8, 16])
                bo_ = lambda ri: Bpw[:, ri].rearrange("p g (j c) -> p g j c", c=16)
                cmulv(bo_(0), bo_(1), bb_(0), bb_(1), pr_(0), pr_(1), [8, 8, 16], [bbb, b0], [bBpw])
                pk0, bk0 = bank()
                pk1, bk1 = bank()
                for g8 in range(8):
                    pk_, bk_ = (pk0, bk0) if g8 < 4 else (pk1, bk1)
                    MM(pk_[0:16, (g8 % 4) * 128:(g8 % 4) * 128 + 128], BBst[:, g8, :], CAst[:, g8, :], g8 % 4 == 0, True, [bBB, bCAp], [bk_],
                       skip_group_check=True)
                CP("scalar", Kpad[0:16, 0:4, 112:240], pk0[0:16, :].rearrange("p (g m) -> p g m", g=4), [bk0], [bKp])
                CP("scalar", Kpad[0:16, 4:8, 112:240], pk1[0:16, :].rearrange("p (g m) -> p g m", g=4), [bk1], [bKp])
                for g8 in range(8):
                    for ri in range(2):
                        pb, bb = bank()
                        TR(pb[:, 0:64], Bpw[0:64, ri, g8, :], identf[0:64, 0:64], [bBpw, bconst], [bb])
                        CP("scalar" if ri else "vector", TBb[:, g8, ri, :], pb[:, 0:64], [bb], [bTB])
                    pb, bb = bank()
                    for j in range(8):
                        w0 = (7 - j) * 16
                        MM(pb[:, 0:128], Wm[0:16, 0, w0:w0 + 128], Kpad[0:16, g8, w0:w0 + 128], j == 0, j == 7, [bKp, bconst], [bb])
                    CP("scalar" if g8 % 2 else "vector", TBt[:, g8, :], pb[:, 0:128], [bb], [bTB])
                DMA(tbl_d[fc], TB, [bTB], [btbl[fc], bTB], sem="c1")

        bA8s = Buf("A8s")
        btbl = [Buf(f"tbl{i}") for i in range(8)]
        phase0()
        P.barrier()
        betb = [Buf(f"etb{i}") for i in range(8)]

        def phase0b():
            cv = Carver()
            b0 = Buf("p0b")
            a8r, a8i = A8s[:, 0, 0, :], A8s[:, 1, 0, :]
            r8 = cv.take([128, 32], F32); inv = cv.take([128, 32], F32); hlf = cv.take([128, 32], F32)
            w1 = cv.take([128, 32], F32); w2 = cv.take([128, 32], F32)
            Uk = cv.take([128, 8, 2, 32], F32)
            TT("vector", w1, a8r, a8r, ALU.mult, [bA8s], [b0])
            TT("vector", w2, a8i, a8i, ALU.mult, [bA8s], [b0])
            TT("vector", w1, w1, w2, ALU.add, [b0], [b0])
            MS("gpsimd", hlf, 0.5, [b0])
            TT("gpsimd", r8, w1, hlf, ALU.pow, [b0], [b0])
            E("vector", lambda h: h.reciprocal(out=inv, in_=r8), [b0], [b0])
            TT("vector", Uk[:, 0, 0, :], a8r, inv, ALU.mult, [b0, bA8s], [b0])
            TT("vector", Uk[:, 0, 1, :], a8i, inv, ALU.mult, [b0, bA8s], [b0])
            for k in range(7):
                ur, ui = Uk[:, k, 0, :], Uk[:, k, 1, :]
                TT("vector", w1, ur, ur, ALU.mult, [b0], [b0])
                TT("vector", w2, ui, ui, ALU.mult, [b0], [b0])
                TT("vector", Uk[:, k + 1, 0, :], w1, w2, ALU.subtract, [b0], [b0])
                TT("vector", w1, ur, ui, ALU.mult, [b0], [b0])
                TT("vector", Uk[:, k + 1, 1, :], w1, w1, ALU.add, [b0], [b0])
            Eb = cv.take([128, 2, 16, NCH], F32)
            Cf = cv.take([128, 16, NCH], F32)
            T1 = cv.take([128, 16, 128], F32); T2 = cv.take([128, 16, 128], F32)
            for batch in range(2):
                ps_ = slice(16 * batch, 16 * batch + 16)
                MS("vector", Eb[:, 0, :, 0:1], 1.0, [b0]); MS("vector", Eb[:, 1, :, 0:1], 0.0, [b0])
                for k in range(8):
                    n = 1 << k
                    Ur = Uk[:, k, 0, ps_].unsqueeze(2).to_broadcast([128, 16, n])
                    Ui = Uk[:, k, 1, ps_].unsqueeze(2).to_broadcast([128, 16, n])
                    lo_r, lo_i = Eb[:, 0, :, 0:n], Eb[:, 1, :, 0:n]
                    TT("vector", T1[:, :, 0:n], lo_r, Ur, ALU.mult, [b0], [b0])
                    TT("vector", T2[:, :, 0:n], lo_i, Ui, ALU.mult, [b0], [b0])
                    TT("vector", Eb[:, 0, :, n:2 * n], T1[:, :, 0:n], T2[:, :, 0:n], ALU.subtract, [b0], [b0])
                    TT("vector", T1[:, :, 0:n], lo_i, Ur, ALU.mult, [b0], [b0])
                    TT("vector", T2[:, :, 0:n], lo_r, Ui, ALU.mult, [b0], [b0])
                    TT("vector", Eb[:, 1, :, n:2 * n], T1[:, :, 0:n], T2[:, :, 0:n], ALU.add, [b0], [b0])
                for ri in range(2):
                    CP("vector", Eb[:, ri, :, 256:NCH].rearrange("p a (s j) -> p a s j", j=8),
                       Eb[:, ri, :, 0:8].unsqueeze(2).to_broadcast([128, 16, 4, 8]), [b0], [b0])
                MS("vector", Cf, 1.0, [b0])
                MS("vector", Cf[:, :, 0:1], 0.0, [b0])
                MS("vector", Cf[:, :, 256:NCH].rearrange("p a (s j) -> p a s j", j=8)[:, :, :, 0:1], 0.0, [b0])
                TT("vector", Cf, Cf, r8[:, ps_].unsqueeze(2).to_broadcast([128, 16, NCH]), ALU.mult, [b0], [b0])
                for q in range(4):
                    fc = 4 * batch + q
                    dv = etb_d[fc].rearrange("p k (a n) -> p k a n", a=4)
                    DMA(dv[:, 0], Eb[:, 0, 4 * q:4 * q + 4, :], [b0], [betb[fc]])
                    DMA(dv[:, 1], Eb[:, 1, 4 * q:4 * q + 4, :], [b0], [betb[fc]])
                    DMA(dv[:, 2], Cf[:, 4 * q:4 * q + 4, :], [b0], [betb[fc]])

        phase0b()
        P.barrier()

        class Ring:
            def __init__(self, cv):
                self.slots = [cv.take([128, 2048], BF16) for _ in range(6)]
                self.bs = [Buf(f"slot{i}") for i in range(6)]
                self.stg = [cv.take([128, 2048], F32) for _ in range(2)]
                self.bstg = [Buf(f"stg{i}") for i in range(2)]
                self.jobs = []
                self.released = []
                self.where = []
                self.casters = []
                self.owner = [None] * 6
                self.nslot = 0
                self.nstg = 0
                self.issued = 0

            def reset_bufs(self):
                self.bs = [Buf(f"slot{i}") for i in range(6)]
                self.bstg = [Buf(f"stg{i}") for i in range(2)]

            def add(self, src, a, b, raw=False, caster="gpsimd"):
                self.jobs.append((src, a, b, raw))
                self.casters.append(caster)
                self.released.append(False)
                self.where.append(None)
                return len(self.jobs) - 1

            def pump(self, limit=None):
                while self.issued < len(self.jobs) and (limit is None or self.issued <= limit):
                    k = self.issued
                    src, a, b, raw = self.jobs[k]
                    t = self.nstg % 2
                    st = self.stg[t][:, 0:a * b].rearrange("p (a b) -> p a b", a=a)
                    if raw:
                        P.emit("sync", lambda h, st=st, src=src: h.dma_start(out=st, in_=src), [], [self.bstg[t]], dma=f"stg{t}")
                        self.where[k] = ("stg", t)
                    else:
                        s = None
                        for q in range(6):
                            c = (self.nslot + q) % 6
                            own = self.owner[c]
                            if own is None or self.released[own]:
                                s = c
                                break
                        if s is None:
                            break
                        sl = self.slots[s][:, 0:a * b].rearrange("p (a b) -> p a b", a=a)
                        P.emit("sync", lambda h, st=st, src=src: h.dma_start(out=st, in_=src), [], [self.bstg[t]], dma=f"stg{t}")
                        CP(self.casters[k], sl, st, [self.bstg[t]], [self.bs[s]])
                        self.owner[s] = k
                        self.where[k] = ("slot", s)
                        self.nslot = s + 1
                    self.nstg += 1
                    self.issued += 1

            def get(self, k):
                assert k < self.issued, (k, self.issued)
                src, a, b, raw = self.jobs[k]
                kind, idx = self.where[k]
                if kind == "stg":
                    return self.stg[idx][:, 0:a * b].rearrange("p (a b) -> p a b", a=a), self.bstg[idx]
                return self.slots[idx][:, 0:a * b].rearrange("p (a b) -> p a b", a=a), self.bs[idx]

            def release(self, k):
                self.released[k] = True

        cvF = Carver()
        ring = Ring(cvF)
        ring_end = cvF.off
        actT = [cvF.take([128, 2, 512], BF16) for _ in range(3)]
        bact = [Buf(f"act{i}") for i in range(3)]
        stmp = [cvF.take([128, 512], F32) for _ in range(2)]
        bstmp = [Buf(f"stmp{i}") for i in range(2)]
        xn = [cvF.take([128, D], BF16) for _ in range(2)]
        bxn = [Buf(f"xn{i}") for i in range(2)]
        gbc = cvF.take([128, D], F32); bgbc = Buf("gbc")
        junk = cvF.take([128, D], BF16); bjunk = Buf("junk")
        ffn_end = cvF.off
        cnt = {"act": 0, "stmp": 0, "xn": 0}

        def norm_T(gain_row):
            DMAU(gbc, gain_row.partition_broadcast(128), [], [bgbc])
            for t in range(NT):
                ACT(junk, X[:, t, :], AF.Square, [bX[t]], [bjunk, bssq], accum_out=ssq[:, t:t + 1])
            ACT(rstd[:], ssq[:], AF.Sqrt, [bssq], [brstd], scale=1.0 / D, bias=EPS)
            E("vector", lambda h: h.reciprocal(out=rstd[:], in_=rstd[:]), [brstd], [brstd])
            for t in range(NT):
                i = cnt["xn"] % 2
                cnt["xn"] += 1
                STT(xn[i], X[:, t, :], rstd[:, t:t + 1], gbc, ALU.mult, ALU.mult, [bX[t], brstd, bgbc], [bxn[i]])
                pb, bb = bank()
                pT = pb[:].bitcast(BF16)
                for kc in range(KC):
                    TR(pT[:, kc * 128:(kc + 1) * 128], xn[i][:, kc * 128:(kc + 1) * 128], ident[:], [bxn[i], bconst], [bb])
                CP("scalar", hT[:, :, t * 128:(t + 1) * 128], pT.rearrange("p (k n) -> p k n", k=KC), [bb], [bhT[t]])

        TBLK = [(0, 4), (4, 4), (8, 4), (12, 4), (16, 2)]

        def ffn(idx, gain_row):
            base = len(ring.jobs)
            for g in range(NGRP):
                c0 = g * FG
                ring.add(wg_d[idx].rearrange("(k p) n -> p k n", p=128)[:, :, c0:c0 + FG], KC, FG)
                ring.add(wu_d[idx].rearrange("(k p) n -> p k n", p=128)[:, :, c0:c0 + FG], KC, FG)
                ring.add(wd_d[idx, c0:c0 + FG, :].rearrange("(k p) n -> p k n", p=128), 2, D)
            ring.pump(limit=base + 5)
            norm_T(gain_row)
            pend = None

            def down(g, tb, ai):
                t0, ntl = TBLK[tb]
                wdv, bwd = ring.get(base + 3 * g + 2)
                for tt in range(ntl):
                    t = t0 + tt
                    for half in range(2):
                        pb, bb = bank()
                        for fc in range(2):
                            MM(pb[:, :], actT[ai][:, fc, tt * 128:(tt + 1) * 128], wdv[:, fc, half * 512:(half + 1) * 512],
                               fc == 0, fc == 1, [bact[ai], bwd], [bb])
                        xs_ = X[:, t, half * 512:(half + 1) * 512]
                        STT(xs_, pb[:, :], 0.5, xs_, ALU.mult, ALU.add, [bb, bX[t]], [bX[t]])
                if tb == len(TBLK) - 1:
                    for q in range(3):
                        ring.release(base + 3 * g + q)
                    ring.pump(limit=base + 3 * g + 8)

            for g in range(NGRP):
                wgv, bwg = ring.get(base + 3 * g)
                wuv, bwu = ring.get(base + 3 * g + 1)
                for tb, (t0, ntl) in enumerate(TBLK):
                    ntok = ntl * 128
                    ai = cnt["act"] % 3
                    cnt["act"] += 1
                    for fc in range(2):
                        pg, bg_ = bank()
                        pu, bu_ = bank()
                        rb = [bhT[t0 + q] for q in range(ntl)]
                        for kc in range(KC):
                            MM(pg[:, 0:ntok], wgv[:, kc, fc * 128:(fc + 1) * 128], hT[:, kc, t0 * 128:t0 * 128 + ntok],
                               kc == 0, kc == KC - 1, [bwg] + rb, [bg_])
                        for kc in range(KC):
                            MM(pu[:, 0:ntok], wuv[:, kc, fc * 128:(fc + 1) * 128], hT[:, kc, t0 * 128:t0 * 128 + ntok],
                               kc == 0, kc == KC - 1, [bwu] + rb, [bu_])
                        si = cnt["stmp"] % 2
                        cnt["stmp"] += 1
                        ACT(stmp[si][:, 0:ntok], pg[:, 0:ntok], AF.Silu, [bg_], [bstmp[si]])
                        TT("vector", actT[ai][:, fc, 0:ntok], stmp[si][:, 0:ntok], pu[:, 0:ntok], ALU.mult,
                           [bstmp[si], bu_], [bact[ai]])
                    if pend is not None:
                        down(*pend)
                    pend = (g, tb, ai)
            down(*pend)

        def proj_jobs(wsrc, col0, ncols, caster="gpsimd"):
            base = len(ring.jobs)
            nj = (ncols + 255) // 256
            for q in range(nj):
                w = min(256, ncols - q * 256)
                ring.add(wsrc.rearrange("(k p) n -> p k n", p=128)[:, :, col0 + q * 256:col0 + q * 256 + w], KC, w, caster=caster)
            ring.pump()
            return base

        def projT(wsrc, col0, ncols, per_tile, tiles=None, caster="gpsimd", base=None):
            nj = (ncols + 255) // 256
            if base is None:
                base = proj_jobs(wsrc, col0, ncols, caster)
            ring.pump()
            assert base + nj - 1 < ring.issued, (base, nj, ring.issued)
            tiles = list(range(NT)) if tiles is None else tiles
            for t in tiles:
                outs = []
                for q in range(nj):
                    w = min(256, ncols - q * 256)
                    wv, bw = ring.get(base + q)
                    if q % 2 == 0:
                        pb, bb = bank()
                    o = pb[:, (q % 2) * 256:(q % 2) * 256 + w]
                    for kc in range(KC):
                        MM(o, hT[:, kc, t * 128:(t + 1) * 128], wv[:, kc, :], kc == 0 and q % 2 == 0, kc == KC - 1,
                           [bhT[t], bw], [bb], skip_group_check=True)
                    outs.append((o, bb, q * 256, w))
                per_tile(t, outs)
            for q in range(nj):
                ring.release(base + q)
            ring.pump()

        ffn(0, ffn_norm[0])
        if upto <= 1:
            return finish(nc, P, X, bX, yp, ys, DMAU)

        allhT = list(bhT)

        def s5_layer():
            norm_T(mix_norm[0])
            P.barrier()
            cv = Carver()
            TBL = [cv.take([128, 3072], BF16) for _ in range(2)]
            bTBL = [Buf("TBL0"), Buf("TBL1")]
            UgB = [cv.take([128, 8, NCH], BF16) for _ in range(2)]
            bUgB = [[Buf(f"Ug{j}_{i}") for i in range(8)] for j in range(2)]
            Hs = cv.take([128, 2, 4, NCH], F32); bH = [Buf("Hr"), Buf("Hi")]
            pt = [cv.take([128, 4, NCH], F32) for _ in range(4)]; bpt = [Buf(f"pt{i}") for i in range(4)]
            Hp = cv.take([128, 2, 4, NCH], BF16); bHp = Buf("Hp")
            Ysb = cv.take([128, 8, NCH], BF16); bY = [Buf(f"Y{i}") for i in range(8)]
            vt, v2 = pt[0], pt[1]; bvt, bv2 = bpt[0], bpt[1]
            H0 = cv.take([128, 2, 32, 4], F32); bH0 = Buf("H0")
            ET = cv.take([128, 3, 4, NCH], F32); bET = Buf("ET")
            m4 = [cv.take([128, 4], F32) for _ in range(4)]; bm4 = Buf("m4")
            Hraw = cv.take([128, 128], F32); bHraw = Buf("Hraw")
            for ri, src in enumerate((h0r_d, h0i_d)):
                DMAU(Hraw, src.rearrange("s (a b) p -> (s a) (b p)", b=2), [], [bHraw])
                pb, bb = bank()
                TR(pb[:, 0:128], Hraw, identf[:], [bHraw, bconst], [bb])
                CP("vector", H0[:, ri, :, :], pb[:, 0:128].rearrange("p (s a) -> p a s", s=4), [bb], [bH0])
            ck(1.1)
            uview = lambda fc: hT[:, fc, :].rearrange("p (n i) -> p i n", i=8)
            HFs = {}

            def stA(fc):
                    tb_, btb = TBL[fc % 2], bTBL[fc % 2]
                    TBt = tb_[:, 0:1024].rearrange("p (g m) -> p g m", g=8)
                    TBb = tb_[:, 1024:2048].rearrange("p (g r m) -> p g r m", g=8, r=2)
                    TBc = tb_[:, 2048:3072].rearrange("p (a r m) -> p a r m", a=4, r=2)
                    uv = uview(fc)
                    prs = slice(4 * fc, 4 * fc + 4)
                    Ug = UgB[fc % 2]; bUg = bUgB[fc % 2]
                    DMAU(tb_, tbl_d[fc], [btbl[fc]], [btb])
                    for g8 in range(8):
                        pb, bb = bank()
                        for i in range(8):
                            MM(pb[:, 0:NCH], Wm[:, g8, 112 - 16 * i:240 - 16 * i], uv[:, i, :], i == 0, i == 7, [bconst] + allhT, [bb])
                        CP("scalar" if g8 % 2 else "vector", Ug[:, g8, :], pb[:, 0:NCH], [bb], [bUg[g8]])
                    ck(1.2)
                    for a in range(4):
                        for ri in range(2):
                            pb, bb = bank()
                            MM(pb[0:64, 0:NCH], TBb[:, 2 * a, ri, :], Ug[:, 2 * a, :], True, True, [btb, bUg[2 * a]], [bb])
                            MM(pb[64:128, 0:NCH], TBb[:, 2 * a + 1, ri, :], Ug[:, 2 * a + 1, :], True, True, [btb, bUg[2 * a + 1]], [bb])
                            CP("scalar" if ri else "vector", Hs[:, ri, a, :], pb[:, 0:NCH], [bb], [bH[ri]])
                    ck(1.3)
                    for a in range(4):
                        pr = 4 * fc + a
                        a8r = A8s[:, 0, 0, pr:pr + 1]; a8i = A8s[:, 1, 0, pr:pr + 1]
                        h0r_, h0i_ = H0[:, 0, pr, :], H0[:, 1, pr, :]
                        hr_ = Hs[:, 0, a, 256:NCH].rearrange("p (s j) -> p s j", j=8)[:, :, 0]
                        hi_ = Hs[:, 1, a, 256:NCH].rearrange("p (s j) -> p s j", j=8)[:, :, 0]
                        TS("gpsimd", m4[0], h0r_, a8r, None, ALU.mult, None, [bH0, bA8s], [bm4])
                        TS("gpsimd", m4[1], h0i_, a8i, None, ALU.mult, None, [bH0, bA8s], [bm4])
                        TS("gpsimd", m4[2], h0r_, a8i, None, ALU.mult, None, [bH0, bA8s], [bm4])
                        TS("gpsimd", m4[3], h0i_, a8r, None, ALU.mult, None, [bH0, bA8s], [bm4])
                        TT("gpsimd", hr_, hr_, m4[0], ALU.add, [bm4, bH[0]], [bH[0]])
                        TT("gpsimd", hr_, hr_, m4[1], ALU.subtract, [bm4, bH[0]], [bH[0]])
                        TT("gpsimd", hi_, hi_, m4[2], ALU.add, [bm4, bH[1]], [bH[1]])
                        TT("gpsimd", hi_, hi_, m4[3], ALU.add, [bm4, bH[1]], [bH[1]])

            def stB(fc):
                    tb_, btb = TBL[fc % 2], bTBL[fc % 2]
                    TBt = tb_[:, 0:1024].rearrange("p (g m) -> p g m", g=8)
                    TBb = tb_[:, 1024:2048].rearrange("p (g r m) -> p g r m", g=8, r=2)
                    TBc = tb_[:, 2048:3072].rearrange("p (a r m) -> p a r m", a=4, r=2)
                    uv = uview(fc)
                    prs = slice(4 * fc, 4 * fc + 4)
                    Ug = UgB[fc % 2]; bUg = bUgB[fc % 2]
                    ck(1.4)
                    DMAU(ET, etb_d[fc].rearrange("p k (a n) -> p k a n", a=4), [betb[fc]], [bET])
                    Er, Ei, Cf_ = ET[:, 0], ET[:, 1], ET[:, 2]
                    T1, T2, Wr, Wi = pt
                    fl = lambda v: v.rearrange("p a n -> p (a n)")
                    TT("vector", T1, Er, Hs[:, 0], ALU.mult, [bET, bH[0]], [bpt[0]])
                    TT("vector", T2, Ei, Hs[:, 1], ALU.mult, [bET, bH[1]], [bpt[1]])
                    TT("vector", Wr, T1, T2, ALU.add, [bpt[0], bpt[1]], [bpt[2]])
                    TT("vector", T1, Er, Hs[:, 1], ALU.mult, [bET, bH[1]], [bpt[0]])
                    TT("vector", T2, Ei, Hs[:, 0], ALU.mult, [bET, bH[0]], [bpt[1]])
                    TT("vector", Wi, T1, T2, ALU.subtract, [bpt[0], bpt[1]], [bpt[3]])
                    E("vector", lambda h: h.tensor_tensor_scan(out=fl(Hs[:, 0]), data0=fl(Cf_), data1=fl(Wr), initial=0.0,
                                                               op0=ALU.mult, op1=ALU.add), [bET, bpt[2]], [bH[0]])
                    E("vector", lambda h: h.tensor_tensor_scan(out=fl(Hs[:, 1]), data0=fl(Cf_), data1=fl(Wi), initial=0.0,
                                                               op0=ALU.mult, op1=ALU.add), [bET, bpt[3]], [bH[1]])
                    TT("vector", T1, Er, Hs[:, 0], ALU.mult, [bET, bH[0]], [bpt[0]])
                    TT("vector", T2, Ei, Hs[:, 1], ALU.mult, [bET, bH[1]], [bpt[1]])
                    TT("vector", Wr, T1, T2, ALU.subtract, [bpt[0], bpt[1]], [bpt[2]])
                    TT("vector", T1, Er, Hs[:, 1], ALU.mult, [bET, bH[1]], [bpt[0]])
                    TT("vector", T2, Ei, Hs[:, 0], ALU.mult, [bET, bH[0]], [bpt[1]])
                    TT("vector", Wi, T1, T2, ALU.add, [bpt[0], bpt[1]], [bpt[3]])
                    HF = [Wr, Wi]; bHF = [bpt[2], bpt[3]]
                    sv_ = lambda ri: HF[ri][:, :, 256:NCH].rearrange("p a (s j) -> p a s j", j=8)
                    ck(1.5)
                    for ri in range(2):
                        eng = "vector" if ri == 0 else "gpsimd"
                        CP(eng, Hp[:, ri, :, 1:256], HF[ri][:, :, 0:255], [bHF[ri]], [bHp])
                        MS(eng, Hp[:, ri, :, 0:1], 0.0, [bHp])
                        hpv = Hp[:, ri, :, 256:NCH].rearrange("p a (s j) -> p a s j", j=8)
                        CP(eng, hpv[:, :, :, 1:8], sv_(ri)[:, :, :, 0:7], [bHF[ri]], [bHp])
                        CP(eng, hpv[:, :, :, 0], H0[:, ri, prs, :], [bH0], [bHp])
                        CP(eng, Hfin[:, ri, 0, prs], HF[ri][:, :, 255], [bHF[ri]], [bHfin])
                        CP(eng, Hfin[:, ri, 1:5, prs].rearrange("p s a -> p a s"), sv_(ri)[:, :, :, 7], [bHF[ri]], [bHfin])

            def stC(fc):
                    tb_, btb = TBL[fc % 2], bTBL[fc % 2]
                    TBt = tb_[:, 0:1024].rearrange("p (g m) -> p g m", g=8)
                    TBb = tb_[:, 1024:2048].rearrange("p (g r m) -> p g r m", g=8, r=2)
                    TBc = tb_[:, 2048:3072].rearrange("p (a r m) -> p a r m", a=4, r=2)
                    uv = uview(fc)
                    prs = slice(4 * fc, 4 * fc + 4)
                    Ug = UgB[fc % 2]; bUg = bUgB[fc % 2]
                    ck(1.6)
                    for g8 in range(8):
                        a, h2 = g8 // 2, g8 % 2
                        rows = slice(64 * h2, 64 * h2 + 64)
                        pb, bb = bank()
                        MM(pb[:, 0:NCH], TBt[:, g8, :], Ug[:, g8, :], True, False, [btb, bUg[g8]], [bb])
                        MM(pb[:, 0:NCH], TBc[rows, a, 0, :], Hp[rows, 0, a, :], False, False, [btb, bHp], [bb])
                        MM(pb[:, 0:NCH], TBc[rows, a, 1, :], Hp[rows, 1, a, :], False, True, [btb, bHp], [bb])
                        CP("scalar" if g8 % 2 else "vector", Ysb[:, g8, :], pb[:, 0:NCH], [bb], [bY[g8]])
                    ck(1.7)
                    for ih in range(2):
                        for ii in range(4):
                            i = 4 * ih + ii
                            pb, bb = bank()
                            for g8 in range(8):
                                MM(pb[:, 0:NCH], Wm[:, i, 112 - 16 * g8:240 - 16 * g8], Ysb[:, g8, :], g8 == 0, g8 == 7, [bconst, bY[g8]], [bb])
                            STT(vt[:, ii, :], uv[:, i, :], Dfm[:, fc:fc + 1], pb[:, 0:NCH], ALU.mult, ALU.add, [bb, bconst] + allhT, [bvt])
                        TT("gpsimd", v2, vt, vt, ALU.mult, [bvt], [bv2])
                        TS("gpsimd", v2, v2, 0.044715, 1.0, ALU.mult, ALU.add, [bv2], [bv2])
                        TT("gpsimd", v2, v2, vt, ALU.mult, [bv2, bvt], [bv2])
                        ACT(v2, v2, AF.Sigmoid, [bv2], [bv2], scale=1.5957691216057308)
                        TT("vector", uv[:, 4 * ih:4 * ih + 4, :], vt, v2, ALU.mult, [bvt, bv2], allhT)

            stA(0)
            for fc in range(8):
                stB(fc)
                if fc + 1 < 8:
                    stA(fc + 1)
                stC(fc)
            ck(1.8)
            Hout = [cv.take([128, 128], F32)[0:32, :] for _ in range(2)]; bHout = [Buf("Hout0"), Buf("Hout1")]
            nho = 0
            for ri, (dp, ds) in enumerate(((psr, ssr), (psi, ssi))):
                for j in range(5):
                    pb, bb = bank()
                    TR(pb[0:32, 0:128], Hfin[:, ri, j, :], identf[:], [bHfin, bconst], [bb])
                    ho, bho = Hout[nho % 2], bHout[nho % 2]
                    nho += 1
                    CP("vector", ho, pb[0:32, 0:128], [bb], [bho])
                    dst = dp if j == 0 else ds[j - 1]
                    DMAU(dst.rearrange("(a b) p -> a (b p)", b=2), ho, [bho], [bho], eng="sync")
            ck(1.9)
            P.barrier()
            ring.reset_bufs()

        def glu():
            bases = []
            for half in range(2):
                bases.append(len(ring.jobs))
                for q in range(2):
                    c0 = half * 512 + q * 256
                    ring.add(w_glu_a.rearrange("(k p) n -> p k n", p=128)[:, :, c0:c0 + 256], KC, 256)
                    ring.add(w_glu_b.rearrange("(k p) n -> p k n", p=128)[:, :, c0:c0 + 256], KC, 256)
            for half in range(2):
                base = bases[half]
                ring.pump()
                for t in range(NT):
                    pa, ba = bank()
                    pb_, bb_ = bank()
                    for q in range(2):
                        wa, bwa = ring.get(base + 2 * q)
                        wb, bwb = ring.get(base + 2 * q + 1)
                        for kc in range(KC):
                            MM(pa[:, q * 256:(q + 1) * 256], hT[:, kc, t * 128:(t + 1) * 128], wa[:, kc, :], kc == 0 and q == 0,
                               kc == KC - 1, [bhT[t], bwa], [ba], skip_group_check=True)
                        for kc in range(KC):
                            MM(pb_[:, q * 256:(q + 1) * 256], hT[:, kc, t * 128:(t + 1) * 128], wb[:, kc, :], kc == 0 and q == 0,
                               kc == KC - 1, [bhT[t], bwb], [bb_], skip_group_check=True)
                    si = cnt["stmp"] % 2
                    cnt["stmp"] += 1
                    ACT(stmp[si], pb_[:, :], AF.Sigmoid, [bb_], [bstmp[si]])
                    TT("vector", stmp[si], stmp[si], pa[:, :], ALU.mult, [bstmp[si], ba], [bstmp[si]])
                    xs_ = X[:, t, half * 512:(half + 1) * 512]
                    TT("gpsimd", xs_, xs_, stmp[si], ALU.add, [bstmp[si], bX[t]], [bX[t]])
                for q in range(4):
                    ring.release(base + q)

        cvK = Carver()
        cvK.off = ffn_end
        ktmp = cvK.take([128, D], F32); bktmp = Buf("ktmp")
        ktmpB = cvK.take([128, D], F32); bktmpB = Buf("ktmpB")
        kout = [cvK.take([128, D], F32) for _ in range(2)]; bkout = [Buf("kout0"), Buf("kout1")]
        kss = cvK.take([128, NH], F32); bkss = Buf("kss")
        lft = cvK.take([128, NH], F32); blft = Buf("lft")
        cnt["kout"] = 0

        def head_rmsnorm(outs, gain_b, dst, bdst, out_engine="gpsimd", ktmp=ktmp, bktmp=bktmp, kss=kss, bkss=bkss):
            for hb in range(2):
                o0, bb = outs[2 * hb][0], outs[2 * hb][1]
                o1 = outs[2 * hb + 1][0]
                for q, o in enumerate((o0, o1)):
                    c = hb * 512 + q * 256
                    ACT(ktmp[:, c:c + 256], o, AF.Square, [bb], [bktmp])
            E("vector", lambda h: h.tensor_reduce(out=kss, in_=ktmp.rearrange("p (h d) -> p h d", d=HD), axis=AX.X, op=ALU.add),
              [bktmp], [bkss])
            ACT(kss, kss, AF.Sqrt, [bkss], [bkss], scale=1.0 / HD, bias=EPS)
            E("vector", lambda h: h.reciprocal(out=kss, in_=kss), [bkss], [bkss])
            for hb in range(2):
                bb = outs[2 * hb][1]
                for q in range(2):
                    o = outs[2 * hb + q][0]
                    c = hb * 512 + q * 256
                    hh = c // HD
                    TT("vector", ktmp[:, c:c + 256].rearrange("p (h d) -> p h d", d=HD), o.rearrange("p (h d) -> p h d", d=HD),
                       kss[:, hh:hh + 4].unsqueeze(2).to_broadcast([128, 4, HD]), ALU.mult, [bb, bkss], [bktmp])
            TT(out_engine, dst.rearrange("p (h d) -> p h d", d=HD), ktmp.rearrange("p (h d) -> p h d", d=HD),
               gain_b.unsqueeze(1).to_broadcast([128, NH, HD]), ALU.mult, [bktmp, bconst], [bdst])

        def cumsum16(src3, bsrc, dst3, bdst):
            pw, bw = bank()
            MM(pw[:, 0:256], trif[:], src3.rearrange("p t h -> p (t h)"), True, True, [bsrc, bconst], [bw])
            pt_, bt_ = bank()
            MM(pt_[:, 0:256], onesf[:], src3.rearrange("p t h -> p h t"), True, True, [bsrc, bconst], [bt_])
            E("vector", lambda h: h.tensor_tensor_scan(out=scn[:], data0=maskT[:], data1=pt_[:, 0:256], initial=0.0,
                                                       op0=ALU.mult, op1=ALU.add), [bt_, bconst], [bscn])
            TT("vector", dst3, scn[:].rearrange("p (h t) -> p t h", t=16), pt_[:, 0:256].rearrange("p (h t) -> p t h", t=16),
               ALU.subtract, [bscn, bt_], [bdst])
            TT("vector", dst3, dst3, pw[:, 0:256].rearrange("p (t h) -> p t h", h=NH), ALU.add, [bw, bdst], [bdst])

        def row_dst(t, dp, ds):
            return dp[t * 128:(t + 1) * 128, :] if t < NPT else ds[(t - NPT) * 128:(t - NPT + 1) * 128, :]

        def kv_phase():
            norm_T(kv_norm)

            def k_tile(t, outs):
                i = cnt["kout"] % 2
                cnt["kout"] += 1
                kt_, bkt_ = (ktmp, bktmp) if i == 0 else (ktmpB, bktmpB)
                head_rmsnorm(outs, knb[:, :], kout[i], bkout[i], out_engine="vector", ktmp=kt_, bktmp=bkt_)
                DMAU(row_dst(t, pk, sk), kout[i], [bkout[i]], [bkout[i]], eng="sync")

            def v_tile(t, outs):
                i = cnt["kout"] % 2
                cnt["kout"] += 1
                for (o, bb, c, w) in outs:
                    CP("scalar" if (c // 512) % 2 else "vector", kout[i][:, c:c + w], o, [bb], [bkout[i]])
                DMAU(row_dst(t, pv, sv), kout[i], [bkout[i]], [bkout[i]], eng="sync")

            def f_tile(t, outs):
                o, bb, c, w = outs[0]
                TT("vector", lft, o, bfb[:, :], ALU.add, [bb, bconst], [blft])
                ACT(lft, lft, AF.Exp, [blft], [blft], scale=-1.0)
                ACT(lft, lft, AF.Ln, [blft], [blft], bias=1.0)
                TS("vector", LF[:, t, :], lft, -1.0, None, ALU.mult, None, [blft], [bLF])

            bF = proj_jobs(w_kvf, 2 * D, NH)
            bK = proj_jobs(w_kvf, 0, D)
            projT(w_kvf, 2 * D, NH, f_tile, base=bF)
            bV = proj_jobs(w_kvf, D, D)
            for q in range(4):
                DMAU(plf[q * 512:(q + 1) * 512, :].rearrange("(t p) h -> p t h", p=128), LF[:, 4 * q:4 * q + 4, :], [bLF], [], eng="sync")
            DMAU(slf.rearrange("(t p) h -> p t h", p=128), LF[:, NPT:NT, :], [bLF], [], eng="sync")
            cumsum16(LF[:, 0:NPT, :], bLF, CKc[:, 0:NPT, :], bCK)
            projT(w_kvf, 0, D, k_tile, base=bK)
            projT(w_kvf, D, D, v_tile, base=bV)

        def attention_layer():
            norm_T(mix_norm[1])
            P.barrier()
            cvA = Carver()
            cvA.off = ring_end
            QT = cvA.take([128, 8, 512], BF16); bQT = Buf("QT")
            qn = cvA.take([128, D], BF16); bqn = Buf("qn")
            KT = [cvA.take([128, 2048], BF16) for _ in range(2)]; bKT = [Buf("KT0"), Buf("KT1")]
            VA = [cvA.take([128, 16, 2, 65], BF16) for _ in range(2)]; bVA = [Buf("VA0"), Buf("VA1")]
            PT = [cvA.take([128, 512], BF16) for _ in range(5)]; bPT = [Buf(f"PT{i}") for i in range(5)]
            biasb = cvA.take([128, 17, NH], F32); bbias = Buf("bias")
            biasb2 = cvA.take([128, 17, NH], F32); bbias2 = Buf("bias2")
            c0s = cvA.take([128, NH], F32); bc0 = Buf("c0")
            osb = [cvA.take([128, 4, 128], BF16) for _ in range(2)]; bosb = [Buf("osb0"), Buf("osb1")]
            rc = cvA.take([128, 8], F32); brc = Buf("rc")
            CL = cvA.take([128, 16, NH], F32); bCL = Buf("CL")
            qtmp = cvA.take([128, D], F32); bqtmp = Buf("qtmp")
            qss = cvA.take([128, NH], F32); bqss = Buf("qss")
            KTn = QT[:, :, 256:384]
            VAn = qn.rearrange("p (h d) -> p h d", d=HD)
            bKTn = bQT; bVAn = bqn
            for i in range(2):
                MS("gpsimd", VA[i][:, :, :, 64:65], 1.0, [bVA[i]])
            onescol = VA[0][:, 0, 0, 64:65]
            st = {"pt": 0, "kv": 0, "o": 0}
            pool["ids"] = [4, 5]
            wq_v = w_q
            wo_v = w_o

            def q_proj(tiles):
                def q_tile(t, outs):
                    head_rmsnorm(outs, qnb[:, :], qn, bqn, out_engine="vector", ktmp=qtmp, bktmp=bqtmp, kss=qss, bkss=bqss)
                    pb, bb = bank()
                    pT = pb[:].bitcast(BF16)
                    for pr in range(8):
                        TR(pT[:, pr * 128:(pr + 1) * 128], qn[:, pr * 128:(pr + 1) * 128], ident[:], [bqn, bconst], [bb])
                    tt = tiles.index(t)
                    CP("scalar", QT[:, :, tt * 128:(tt + 1) * 128], pT.rearrange("p (k n) -> p k n", k=8), [bb], [bQT])
                pool["ids"] = list(range(6))
                projT(wq_v, 0, D, q_tile, tiles=tiles, caster="vector")
                pool["ids"] = [4, 5]

            def o_proj(tiles, base=None):
                def o_tile(t, outs):
                    for (o, bb, c, w) in outs:
                        xs_ = X[:, t, c:c + w]
                        TT("vector", xs_, xs_, o, ALU.add, [bb, bX[t]], [bX[t]])
                pool["ids"] = list(range(6))
                projT(wo_v, 0, D, o_tile, tiles=tiles, caster="vector", base=base)
                pool["ids"] = [4, 5]

            def load_kv(ksrc, vsrc, nkt, pair):
                i = st["kv"] % 2
                st["kv"] += 1
                cs = slice(pair * 128, (pair + 1) * 128)
                jk = ring.add(ksrc.rearrange("(t p) c -> p t c", p=128)[:, 0:nkt, cs], nkt, 128, caster="vector")
                jv = ring.add(vsrc.rearrange("(t p) c -> p t c", p=128)[:, 0:nkt, cs], nkt, 128, raw=True)
                ring.pump()
                kv_, bk_ = ring.get(jk)
                for g0 in range(0, nkt, 8):
                    n = min(8, nkt - g0)
                    pb, bb = bank()
                    pT = pb[:].bitcast(BF16)
                    for q in range(n):
                        TR(pT[:, q * 128:(q + 1) * 128], kv_[:, g0 + q, :], ident[:], [bk_, bconst], [bb])
                    CP("vector", KT[i][:, g0 * 128:(g0 + n) * 128], pT[:, 0:n * 128], [bb], [bKT[i]])
                ring.release(jk)
                vv_, bv_ = ring.get(jv)
                CP("vector", VA[i][:, 0:nkt, :, 0:64], vv_.rearrange("p t (h d) -> p t h d", h=2), [bv_], [bVA[i]])
                ring.release(jv)
                return i

            for b in range(4):
                tiles = [4 * b + q for q in range(4)]
                nkt = 4 * b + 4
                kvb = {}

                def ensure_kv(pair, nkt=nkt):
                    if pair < 8 and pair not in kvb:
                        kvb[pair] = load_kv(pk, pv, nkt, pair)

                ensure_kv(0)
                ensure_kv(1)
                q_proj(tiles)
                bo_w = proj_jobs(wo_v, 0, D, caster="vector")
                pb, bb = bank()
                MM(pb[:, 0:NH], sel64[:], CKc[:, 4 * b + 2, :], True, True, [bCK, bconst], [bb])
                CP("vector", c0s, pb[:, 0:NH], [bb], [bc0])
                TT("vector", biasb[:, 0:nkt, :], c0s.unsqueeze(1).to_broadcast([128, nkt, NH]), CKc[:, 0:nkt, :], ALU.subtract,
                   [bc0, bCK], [bbias])
                tasks = [dict(pair=pair, h2=h2, kt=kt) for pair in range(8) for kt in range(nkt) for h2 in range(2)]
                hs = {0: {}, 1: {}, "oi": 0}

                def s1(T, b=b):
                    pair, h2, kt = T["pair"], T["h2"], T["kt"]
                    i = kvb[pair]
                    rows = slice(64 * h2, 64 * h2 + 64)
                    r = kt - 4 * b
                    q0 = max(r, 0) * 128
                    ps_s, bs_ = sbank()
                    MM(ps_s[:, q0:512], KT[i][rows, kt * 128:(kt + 1) * 128], QT[rows, pair, q0:512], True, True,
                       [bKT[i], bQT], [bs_])
                    T.update(ps=ps_s, bs=bs_, q0=q0, r=r, i=i)

                def s2(T):
                    pair, h2, kt = T["pair"], T["h2"], T["kt"]
                    head = 2 * pair + h2
                    q0, r = T["q0"], T["r"]
                    pi = st["pt"] % 5
                    st["pt"] += 1
                    T["pi"] = pi
                    ACT(PT[pi][:, q0:512], T["ps"][:, q0:512], AF.Exp, [T["bs"], bbias], [bPT[pi]], scale=SCALE,
                        bias=biasb[:, kt, head:head + 1])
                    if r >= 0:
                        TT("gpsimd", PT[pi][:, q0:q0 + 128], PT[pi][:, q0:q0 + 128], trib[:], ALU.mult, [bPT[pi], bconst], [bPT[pi]])

                def s3(T, b=b, nkt=nkt, tiles=tiles):
                    pair, h2, kt = T["pair"], T["h2"], T["kt"]
                    i, pi, r = T["i"], T["pi"], T["r"]
                    if kt == 0:
                        hs[h2]["pbo"], hs[h2]["bbo"] = acc_bank()
                        if h2 == 0:
                            hs["oi"] = st["o"] % 2
                            st["o"] += 1
                    pbo, bbo, oi = hs[h2]["pbo"], hs[h2]["bbo"], hs["oi"]
                    for qt in range(max(r, 0), 4):
                        MM(pbo[:, qt * 65:(qt + 1) * 65], PT[pi][:, qt * 128:(qt + 1) * 128], VA[i][:, kt, h2, :],
                           kt == 0 and qt == 0, kt == 4 * b + qt, [bPT[pi], bVA[i]], [bbo], skip_group_check=True)
                    if kt == nkt - 1:
                        ov = pbo[:, 0:260].rearrange("p (q e) -> p q e", e=65)
                        E("vector", lambda h, ov=ov: h.reciprocal(out=rc[:, 0:4], in_=ov[:, :, 64]), [bbo], [brc])
                        TT("vector", osb[oi][:, :, 64 * h2:64 * h2 + 64], ov[:, :, 0:64],
                           rc[:, 0:4].unsqueeze(2).to_broadcast([128, 4, 64]), ALU.mult, [bbo, brc], [bosb[oi]])
                        if h2 == 1:
                            pb, bb = bank()
                            pT = pb[:].bitcast(BF16)
                            for qt in range(4):
                                TR(pT[:, qt * 128:(qt + 1) * 128], osb[oi][:, qt, :], ident[:], [bosb[oi], bconst], [bb])
                            CP("scalar", hT[:, pair, 512 * b:512 * b + 512], pT[:, 0:512], [bb], [bhT[q] for q in tiles])
                            ensure_kv(pair + 2)

                units = [tasks[2 * u:2 * u + 2] for u in range(len(tasks) // 2)]
                LA = 1
                for n in range(len(units) + LA):
                    if n < len(units):
                        for T in units[n]:
                            s1(T)
                    if n - LA >= 0:
                        for T in units[n - LA]:
                            s2(T)
                        for T in units[n - LA]:
                            s3(T)
                o_proj(tiles, base=bo_w)
            ck(5.5)

            q_proj([NPT, NPT + 1])
            for j in range(2):
                t = NPT + j
                jk = ring.add(sk[j * 128:(j + 1) * 128, :].rearrange("p (a c) -> p a c", a=8), 8, 128)
                jv = ring.add(sv[j * 128:(j + 1) * 128, :].rearrange("p (a c) -> p a c", a=8), 8, 128)
                ring.pump()
                kn_, bkn = ring.get(jk)
                pb, bb = bank()
                pT = pb[:].bitcast(BF16)
                for pr in range(8):
                    TR(pT[:, pr * 128:(pr + 1) * 128], kn_[:, pr, :], ident[:], [bkn, bconst], [bb])
                CP("scalar", KTn, pT.rearrange("p (k n) -> p k n", k=8), [bb], [bKTn])
                ring.release(jk)
                vn_, bvn = ring.get(jv)
                CP("gpsimd", qn, vn_.rearrange("p a c -> p (a c)"), [bvn], [bVAn])
                ring.release(jv)
                def prologue(s_):
                    j_, half_ = divmod(s_, 2)
                    t_ = NPT + j_
                    rt_ = slice(64 * half_, 64 * half_ + 64)
                    bz, bbz = (biasb, bbias) if s_ % 2 == 0 else (biasb2, bbias2)
                    for q in range(4):
                        DMAU(CL[:, 4 * q:4 * q + 4, :], clf_d[s_, q * 512:(q + 1) * 512, :].rearrange("(t p) h -> p t h", p=128), [], [bCL])
                    cumsum16(CL[:, :, :], bCL, CL[:, :, :], bCL)
                    pb2, bb2 = bank()
                    MM(pb2[:, 0:NH], tri2f[:], LF[:, t_, :], True, True, [bLF, bconst], [bb2])
                    TT("vector", CKc[rt_, t_, :], pb2[rt_, 0:NH], scn[rt_, :].rearrange("p (h t) -> p h t", t=16)[:, :, 15], ALU.add,
                       [bb2, bscn], [bCK])
                    pb3, bb3 = bank()
                    MM(pb3[:, 0:NH], (sel32 if half_ == 0 else sel96)[:], CKc[:, t_, :], True, True, [bCK, bconst], [bb3])
                    CP("vector", c0s, pb3[:, 0:NH], [bb3], [bc0])
                    TT("vector", bz[:, 0:16, :], c0s.unsqueeze(1).to_broadcast([128, 16, NH]), CL[:, :, :], ALU.subtract,
                       [bc0, bCL], [bbz])
                    TT("vector", bz[:, 16, :], c0s, CKc[:, t_, :], ALU.subtract, [bc0, bCK], [bbz])

                if j == 0:
                    prologue(0)
                for half in range(2):
                    s_ = 2 * j + half
                    rt = slice(64 * half, 64 * half + 64)
                    biasb_s, bbias_s = (biasb, bbias) if s_ % 2 == 0 else (biasb2, bbias2)
                    qc = slice(s_ * 64, s_ * 64 + 64)
                    kvb = {}

                    def ensure_kv(pair, s_=s_):
                        if pair < 8 and pair not in kvb:
                            kvb[pair] = load_kv(ck_d[s_], cv_d[s_], 16, pair)

                    ensure_kv(0)
                    ensure_kv(1)
                    tasks = [dict(pair=pair, h2=h2, kt=kt) for pair in range(8) for kt in range(17) for h2 in range(2)]
                    hs = {0: {}, 1: {}, "oi": 0}

                    def s1(T, qc=qc, rt=rt, half=half):
                        pair, h2, kt = T["pair"], T["h2"], T["kt"]
                        i = kvb[pair]
                        rows = slice(64 * h2, 64 * h2 + 64)
                        ps_s, bs_ = sbank()
                        if kt < 16:
                            MM(ps_s[:, 0:64], KT[i][rows, kt * 128:(kt + 1) * 128], QT[rows, pair, qc], True, True, [bKT[i], bQT], [bs_])
                        else:
                            MM(ps_s[rt, 0:64], KTn[rows, pair, 64 * half:64 * half + 64], QT[rows, pair, qc], True, True, [bKTn, bQT], [bs_])
                        T.update(ps=ps_s, bs=bs_, i=i)

                    def s2(T, rt=rt, biasb_s=biasb_s, bbias_s=bbias_s):
                        pair, h2, kt = T["pair"], T["h2"], T["kt"]
                        head = 2 * pair + h2
                        pi = st["pt"] % 5
                        st["pt"] += 1
                        T["pi"] = pi
                        if kt < 16:
                            ACT(PT[pi][:, 0:64], T["ps"][:, 0:64], AF.Exp, [T["bs"], bbias_s], [bPT[pi]], scale=SCALE,
                                bias=biasb_s[:, kt, head:head + 1])
                        else:
                            ACT(PT[pi][rt, 0:64], T["ps"][rt, 0:64], AF.Exp, [T["bs"], bbias_s], [bPT[pi]], scale=SCALE,
                                bias=biasb_s[rt, 16, head:head + 1])
                            TT("gpsimd", PT[pi][rt, 0:64], PT[pi][rt, 0:64], mask64[rt, :], ALU.mult, [bPT[pi], bconst], [bPT[pi]])

                    def s3(T, rt=rt, s_=s_, t=t):
                        pair, h2, kt = T["pair"], T["h2"], T["kt"]
                        head = 2 * pair + h2
                        i, pi = T["i"], T["pi"]
                        if kt == 0:
                            hs[h2]["pbo"], hs[h2]["bbo"] = acc_bank()
                            if h2 == 0:
                                hs["oi"] = st["o"] % 2
                                st["o"] += 1
                        pbo, bbo, oi = hs[h2]["pbo"], hs[h2]["bbo"], hs["oi"]
                        if kt < 16:
                            MM(pbo[0:64, 0:65], PT[pi][:, 0:64], VA[i][:, kt, h2, :], kt == 0, False, [bPT[pi], bVA[i]], [bbo],
                               skip_group_check=True)
                            return
                        MM(pbo[0:64, 0:64], PT[pi][rt, 0:64], VAn[rt, head, :], False, False, [bPT[pi], bVAn], [bbo], skip_group_check=True)
                        MM(pbo[0:64, 64:65], PT[pi][rt, 0:64], onescol[rt, :], False, True, [bPT[pi], bVA[0]], [bbo], skip_group_check=True)
                        E("vector", lambda h, pbo=pbo: h.reciprocal(out=rc[0:64, 0:1], in_=pbo[0:64, 64:65]), [bbo], [brc])
                        TS("vector", osb[oi][0:64, 0, 64 * h2:64 * h2 + 64], pbo[0:64, 0:64], rc[0:64, 0:1], None, ALU.mult, None,
                           [bbo, brc], [bosb[oi]])
                        if h2 == 1:
                            pb, bb = bank()
                            pT = pb[:].bitcast(BF16)
                            TR(pT[:, 0:64], osb[oi][0:64, 0, :], ident[0:64, 0:64], [bosb[oi], bconst], [bb])
                            CP("scalar", hT[:, pair, 2048 + 64 * s_:2048 + 64 * s_ + 64], pT[:, 0:64], [bb], [bhT[t]])
                            ensure_kv(pair + 2)

                    units = [tasks[2 * u:2 * u + 2] for u in range(len(tasks) // 2)]
                    LA = 1
                    for n in range(len(units) + LA):
                        if n == len(units) // 2 and s_ + 1 < 4:
                            prologue(s_ + 1)
                        if n < len(units):
                            for T in units[n]:
                                s1(T)
                        if n - LA >= 0:
                            for T in units[n - LA]:
                                s2(T)
                            for T in units[n - LA]:
                                s3(T)
            o_proj([NPT, NPT + 1])
            pool["ids"] = list(range(8))
            P.barrier()
            ring.reset_bufs()

        def ck(level):
            if upto <= level:
                raise _Stop()

        try:
            s5_layer()
            glu()
            ck(2)
            ffn(1, ffn_norm[1])
            ck(3)
            kv_phase()
            ck(4)
            ffn(2, ffn_norm[2])
            ck(5)
            attention_layer()
            ck(6)
            ffn(3, ffn_norm[3])
            raise _Stop()
        except _Stop:
            return finish(nc, P, X, bX, yp, ys, DMAU)

    return nc


def finish(nc, P, X, bX, yp, ys, DMAU):
    for t in range(NPT):
        DMAU(yp[t * 128:(t + 1) * 128, :], X[:, t, :], [bX[t]], [], eng="sync")
    for t in range(2):
        DMAU(ys[t * 128:(t + 1) * 128, :], X[:, NPT + t, :], [bX[NPT + t]], [], eng="sync")
    P.finalize()
    return nc


_NC_CACHE = {}


def _in_maps(inp, cores):
    f = lambda a: np.ascontiguousarray(np.asarray(a, dtype=np.float32))
    shared = {
        "ffn_norm": f(inp["ffn_norm"]).reshape(4, D),
        "wg": f(inp["w_ffn_gate"]).reshape(4, D, DFF),
        "wu": f(inp["w_ffn_up"]).reshape(4, D, DFF),
        "wd": f(inp["w_ffn_down"]).reshape(4, DFF, D),
        "mix_norm": f(inp["mix_norm"]),
        "a_re": f(inp["ssm_a_re"])[0], "a_im": f(inp["ssm_a_im"])[0], "log_dt": f(inp["ssm_log_dt"])[0],
        "b_re": f(inp["ssm_b_re"])[0], "b_im": f(inp["ssm_b_im"])[0],
        "c_re": f(inp["ssm_c_re"])[0], "c_im": f(inp["ssm_c_im"])[0],
        "ssm_d": f(inp["ssm_d"])[0],
        "w_glu_a": f(inp["w_glu_a"])[0], "w_glu_b": f(inp["w_glu_b"])[0],
        "kv_norm": f(inp["kv_norm"]), "w_kvf": f(inp["w_kvf"]), "b_f": f(inp["b_f"]), "k_norm": f(inp["k_norm"]),
        "w_q": f(inp["w_q"])[0], "q_norm": f(inp["q_norm"])[0], "w_o": f(inp["w_o"])[0],
    }
    xp = f(inp["x_prompt"]); xs = f(inp["x_sample"])
    ck = f(inp["cache_k"]); cv = f(inp["cache_v"]); clf = f(inp["cache_logf"])
    h0r = f(inp["state_ssm_re"]); h0i = f(inp["state_ssm_im"])
    maps = []
    for c in cores:
        m = dict(shared)
        m["xp"] = xp[c]
        m["xs"] = xs[4 * c:4 * c + 4].reshape(256, D)
        m["ck"] = ck[4 * c:4 * c + 4].reshape(4, S, D)
        m["cv"] = cv[4 * c:4 * c + 4].reshape(4, S, D)
        m["clf"] = clf[4 * c:4 * c + 4]
        m["h0r"] = h0r[4 * c:4 * c + 4, 0]
        m["h0i"] = h0i[4 * c:4 * c + 4, 0]
        maps.append(m)
    return maps


def kernel(**inp):
    if "nc" not in _NC_CACHE:
        _NC_CACHE["nc"] = build_program()
    nc = _NC_CACHE["nc"]
    cores = list(range(8))
    res = run_bass_kernel_spmd(nc, _in_maps(inp, cores), core_ids=cores).results
    cat = lambda k, shp: np.stack([np.asarray(r[k], dtype=np.float32) for r in res]).reshape(shp)
    y_p = cat("yp", (8, S, D))
    y_s = cat("ys", (32, 64, D))
    p_sr = cat("psr", (8, 1, 64, 64)); p_si = cat("psi", (8, 1, 64, 64))
    p_k = cat("pk", (8, S, NH, HD)); p_v = cat("pv", (8, S, NH, HD)); p_lf = cat("plf", (8, S, NH))
    s_sr = cat("ssr", (32, 1, 64, 64)); s_si = cat("ssi", (32, 1, 64, 64))
    s_k = cat("sk", (32, 64, NH, HD)); s_v = cat("sv", (32, 64, NH, HD)); s_lf = cat("slf", (32, 64, NH))
    return (y_p, y_s, p_sr, p_si, p_k, p_v, p_lf, s_sr, s_si, s_k, s_v, s_lf)
```

```python
import math
import numpy as np
from contextlib import ExitStack
import concourse.bass as bass
import concourse.mybir as mybir
from concourse.bass_utils import run_bass_kernel_spmd

F32 = mybir.dt.float32
BF16 = mybir.dt.bfloat16
ALU = mybir.AluOpType
AF = mybir.ActivationFunctionType
AX = mybir.AxisListType

D = 1024
KC = 8
DFF = 4096
FG = 256
NGRP = DFF // FG
NT = 18
NTOK = NT * 128
NPT = 16
S = 2048
NH = 16
HD = 64
NCH = NTOK // 8
EPS = 1e-6
SCALE = 1.0 / 8.0
PI = math.pi


class Buf:
    __slots__ = ("name", "w", "r")

    def __init__(self, name=""):
        self.name = name
        self.w = None
        self.r = []


class Stream:
    def __init__(self, name, handle, sem):
        self.name = name
        self.h = handle
        self.sem = sem
        self.items = []
        self.n = 0
        self.waited = set()
        self.known = {}


class Prog:
    def __init__(self, nc, es):
        self.nc = nc
        self.es = es
        self.streams = {}
        for nm in ("tensor", "vector", "scalar", "gpsimd", "sync"):
            sem = es.enter_context(nc.semaphore("sem_" + nm))
            self.streams[nm] = Stream(nm, getattr(nc, nm), sem)
        self.dma_sems = {}
        self.nbank = 0

    def dma_sem(self, name):
        if name not in self.dma_sems:
            self.dma_sems[name] = [self.es.enter_context(self.nc.semaphore("dsem_" + name)), 0]
        return self.dma_sems[name]

    def _wait(self, st, sp, raw):
        if sp is None:
            return
        if sp[0] == 'c':
            _, s2, idx = sp
            if s2 is st and not raw and st.name == "tensor":
                return
            key = s2.name
            if st.known.get(key, -1) >= idx:
                return
            st.known[key] = idx
            s2.waited.add(idx)
            st.items.append(('wc', s2, idx))
        else:
            _, sem, val = sp
            key = id(sem)
            if st.known.get(key, -1) >= val:
                return
            st.known[key] = val
            st.items.append(('wd', sem, val))

    def emit(self, eng, fn, reads=(), writes=(), dma=None):
        st = self.streams[eng]
        for b in reads:
            self._wait(st, b.w, True)
        for b in writes:
            self._wait(st, b.w, False)
            for sp in b.r:
                self._wait(st, sp, False)
        idx = st.n
        st.n += 1
        if dma is not None:
            d = self.dma_sem(dma)
            d[1] += 16
            sp = ('d', d[0], d[1])
            st.items.append(('dma', fn, d[0]))
        else:
            sp = ('c', st, idx)
            st.items.append(('ins', fn, idx))
        for b in reads:
            b.r.append(sp)
            if len(b.r) > 16:
                last = {}
                for q in b.r:
                    k = q[1].name if q[0] == 'c' else id(q[1])
                    if k not in last or last[k][2] < q[2]:
                        last[k] = q
                b.r = list(last.values())
        for b in writes:
            b.w = sp
            b.r = []
        return sp

    def barrier(self):
        sps = []
        for st in self.streams.values():
            if st.name != "sync" and st.n > 0:
                for it in reversed(st.items):
                    if it[0] == 'ins':
                        sps.append(('c', st, it[2]))
                        break
        for sem, val in self.dma_sems.values():
            if val > 0:
                sps.append(('d', sem, val))
        for st in self.streams.values():
            for sp in sps:
                self._wait(st, sp, True)

    def finalize(self):
        nc = self.nc
        self.barrier()
        ranks = {}
        for st in self.streams.values():
            ranks[st.name] = {idx: k + 1 for k, idx in enumerate(sorted(st.waited))}
        with nc.Block() as block:
            def mk(st):
                def body(h):
                    for it in st.items:
                        if it[0] == 'wc':
                            h.wait_ge(it[1].sem, ranks[it[1].name][it[2]])
                        elif it[0] == 'wd':
                            h.wait_ge(it[1], it[2])
                        elif it[0] == 'dma':
                            it[1](h).then_inc(it[2], 16)
                        else:
                            ins = it[1](h)
                            if it[2] in st.waited:
                                ins.then_inc(st.sem, 1)
                return body
            for nm, st in self.streams.items():
                if st.items:
                    getattr(block, nm)(mk(st))


class _Stop(Exception):
    pass


def build_program(upto=99):
    nc = bass.Bass("TRN2", target_bir_lowering=False)
    es = ExitStack()
    with es:
        es.enter_context(nc.allow_non_contiguous_dma(reason="small strided parameter/state loads"))
        es.enter_context(nc.allow_low_precision(reason="bf16 matmul operands, fp32 accumulation"))
        P = Prog(nc, es)

        def din(name, shape):
            return nc.dram_tensor(name, list(shape), F32, kind="ExternalInput").ap()

        def dout(name, shape, dt=F32):
            return nc.dram_tensor(name, list(shape), dt, kind="ExternalOutput").ap()

        xp = din("xp", [S, D]); xs = din("xs", [256, D])
        ck_d = din("ck", [4, S, D]); cv_d = din("cv", [4, S, D]); clf_d = din("clf", [4, S, NH])
        h0r_d = din("h0r", [4, 64, 64]); h0i_d = din("h0i", [4, 64, 64])
        ffn_norm = din("ffn_norm", [4, D])
        wg_d = din("wg", [4, D, DFF]); wu_d = din("wu", [4, D, DFF]); wd_d = din("wd", [4, DFF, D])
        mix_norm = din("mix_norm", [2, D])
        a_re = din("a_re", [64, 64]); a_im = din("a_im", [64, 64]); log_dt = din("log_dt", [64])
        b_re = din("b_re", [64, 64, 16]); b_im = din("b_im", [64, 64, 16])
        c_re = din("c_re", [64, 16, 64]); c_im = din("c_im", [64, 16, 64])
        ssm_d = din("ssm_d", [D])
        w_glu_a = din("w_glu_a", [D, D]); w_glu_b = din("w_glu_b", [D, D])
        kv_norm = din("kv_norm", [D]); w_kvf = din("w_kvf", [D, 2064]); b_f = din("b_f", [NH])
        k_norm = din("k_norm", [HD]); w_q = din("w_q", [D, D]); q_norm = din("q_norm", [HD]); w_o = din("w_o", [D, D])

        yp = dout("yp", [S, D]); ys = dout("ys", [256, D])
        psr = dout("psr", [64, 64]); psi = dout("psi", [64, 64])
        pk = dout("pk", [S, D]); pv = dout("pv", [S, D]); plf = dout("plf", [S, NH])
        ssr = dout("ssr", [4, 64, 64]); ssi = dout("ssi", [4, 64, 64])
        sk = dout("sk", [256, D]); sv = dout("sv", [256, D]); slf = dout("slf", [256, NH])
        tbl_d = dout("scr_tbl", [8, 128, 3072], BF16)
        etb_d = dout("scr_etb", [8, 128, 3, 4 * NCH])

        def sb(name, shape, dt):
            return es.enter_context(nc.sbuf_tensor(name, list(shape), dt))

        X = sb("X", [128, NT, D], F32)
        bX = [Buf(f"X{t}") for t in range(NT)]
        hT = sb("hT", [128, KC, NTOK], BF16)
        bhT = [Buf(f"hT{t}") for t in range(NT)]
        ident = sb("ident", [128, 128], BF16); identf = sb("identf", [128, 128], F32)
        bconst = Buf("const")
        Wm = sb("Wm", [128, 8, 240], BF16)
        trif = sb("trif", [128, 128], F32)
        tri2f = sb("tri2f", [128, 128], F32)
        onesf = sb("onesf", [128, 128], F32)
        onesA = sb("onesA", [128, 128], F32)
        onesB = sb("onesB", [128, 128], F32)
        sel64 = sb("sel64", [128, 128], F32)
        sel32 = sb("sel32", [128, 128], F32)
        sel96 = sb("sel96", [128, 128], F32)
        trib = sb("trib", [128, 128], BF16)
        mask64 = sb("mask64", [128, 64], BF16)
        rstd = sb("rstd", [128, NT], F32); brstd = Buf("rstd")
        ssq = sb("ssq", [128, NT], F32); bssq = Buf("ssq")
        LF = sb("LF", [128, NT, NH], F32); bLF = Buf("LF")
        maskT = sb("maskT", [128, NH * 16], F32)
        scn = sb("scn", [128, NH * 16], F32); bscn = Buf("scn")
        CKc = sb("CKc", [128, NT, NH], F32); bCK = Buf("CK")
        Dfm = sb("Dfm", [128, KC], F32)
        bfb = sb("bfb", [128, NH], F32)
        knb = sb("knb", [128, HD], F32); qnb = sb("qnb", [128, HD], F32)
        Hfin = sb("Hfin", [128, 2, 5, 32], F32); bHfin = Buf("Hfin")
        A8s = sb("A8s", [128, 2, 8, 32], F32)
        ARENA_F32 = 20736
        arena = sb("arena", [128, ARENA_F32], F32)
        ps = [es.enter_context(nc.psum_tensor(f"ps{i}", [128, 512], F32)) for i in range(8)]
        bps = [Buf(f"ps{i}") for i in range(8)]

        pool = {"ids": list(range(8)), "acc": 0, "s": 0}

        def bank():
            ids = pool["ids"]
            i = ids[P.nbank % len(ids)]
            P.nbank += 1
            return ps[i], bps[i]

        def sbank():
            i = pool["s"] % 4
            pool["s"] += 1
            return ps[i], bps[i]

        def acc_bank():
            i = 6 + pool["acc"] % 2
            pool["acc"] += 1
            return ps[i], bps[i]

        class Carver:
            def __init__(self):
                self.off = 0

            def take(self, shape, dt):
                n = int(np.prod(shape[1:]))
                words = n if dt == F32 else (n + 1) // 2
                words = (words + 7) // 8 * 8
                assert self.off + words <= ARENA_F32, (self.off, words)
                v = arena[:, self.off:self.off + words]
                self.off += words
                if dt != F32:
                    v = v.bitcast(dt)[:, 0:n]
                else:
                    v = v[:, 0:n]
                if len(shape) == 2:
                    return v
                names = " ".join(f"d{i}" for i in range(len(shape) - 1))
                kw = {f"d{i}": shape[i + 1] for i in range(len(shape) - 1)}
                return v.rearrange(f"p ({names}) -> p {names}", **kw)

        def E(eng, fn, R=(), W=()):
            return P.emit(eng, fn, R, W)

        def MM(out, lhsT, rhs, start, stop, R, W, **kw):
            return P.emit("tensor", lambda h: h.matmul(out, lhsT=lhsT, rhs=rhs, start=start, stop=stop, **kw), R, W)

        def TR(out, in_, idt, R, W):
            return P.emit("tensor", lambda h: h.transpose(out=out, in_=in_, identity=idt), R, W)

        def ACT(out, in_, func, R, W, **kw):
            return P.emit("scalar", lambda h: h.activation(out=out, in_=in_, func=func, **kw), R, W)

        def TT(eng, out, in0, in1, op, R, W):
            return P.emit(eng, lambda h: h.tensor_tensor(out=out, in0=in0, in1=in1, op=op), R, W)

        def TS(eng, out, in0, s1, s2, op0, op1, R, W):
            if op1 is None:
                return P.emit(eng, lambda h: h.tensor_scalar(out=out, in0=in0, scalar1=s1, scalar2=None, op0=op0), R, W)
            return P.emit(eng, lambda h: h.tensor_scalar(out=out, in0=in0, scalar1=s1, scalar2=s2, op0=op0, op1=op1), R, W)

        def STT(out, in0, scalar, in1, op0, op1, R, W):
            return P.emit("vector", lambda h: h.scalar_tensor_tensor(out=out, in0=in0, scalar=scalar, in1=in1, op0=op0, op1=op1), R, W)

        def CP(eng, out, in_, R, W):
            if eng == "scalar":
                return P.emit(eng, lambda h: h.copy(out=out, in_=in_), R, W)
            return P.emit(eng, lambda h: h.tensor_copy(out=out, in_=in_), R, W)

        def MS(eng, ap, val, W):
            return P.emit(eng, lambda h: h.memset(ap, val), (), W)

        udma = [0]
        NU = 16

        def DMAU(out, in_, R, W, eng="sync"):
            k = udma[0] % NU
            udma[0] += 1
            name = f"u{k}"
            d = P.dma_sem(name)
            st = P.streams[eng]
            if d[1] > 0:
                P._wait(st, ('d', d[0], d[1]), True)
            return P.emit(eng, lambda h: h.dma_start(out=out, in_=in_), R, W, dma=name)

        def DMA(out, in_, R, W, eng="sync", sem=None):
            return DMAU(out, in_, R, W, eng=eng)

        def affine(out, in_, pattern, cmp, fill, base, cm, R, W):
            return P.emit("gpsimd", lambda h: h.affine_select(out=out, in_=in_, pattern=pattern, compare_op=cmp,
                                                              fill=fill, base=base, channel_multiplier=cm), R, W)

        MS("gpsimd", identf[:], 1.0, [bconst])
        affine(identf[:], identf[:], [[1, 128]], ALU.is_equal, 0.0, 0, -1, [bconst], [bconst])
        CP("gpsimd", ident[:], identf[:], [bconst], [bconst])
        MS("gpsimd", onesf[:], 1.0, [bconst])
        affine(trif[:], onesf[:], [[1, 128]], ALU.is_ge, 0.0, 0, -1, [bconst], [bconst])
        CP("gpsimd", tri2f[:], trif[:], [bconst], [bconst])
        MS("gpsimd", tri2f[0:64, 64:128], 0.0, [bconst])
        MS("gpsimd", onesA[:], 0.0, [bconst]); MS("gpsimd", onesA[:, 0:64], 1.0, [bconst])
        MS("gpsimd", onesB[:], 0.0, [bconst]); MS("gpsimd", onesB[:, 64:128], 1.0, [bconst])
        affine(sel64[:], onesf[:], [[0, 128]], ALU.is_equal, 0.0, -64, 1, [bconst], [bconst])
        affine(sel32[:], onesf[:], [[0, 128]], ALU.is_equal, 0.0, -32, 1, [bconst], [bconst])
        affine(sel96[:], onesf[:], [[0, 128]], ALU.is_equal, 0.0, -96, 1, [bconst], [bconst])
        CP("gpsimd", trib[:], trif[:], [bconst], [bconst])
        MS("gpsimd", Wm[:], 0.0, [bconst])
        for q in range(8):
            CP("gpsimd", Wm[:, q, 112:128], identf[:, 16 * q:16 * q + 16], [bconst], [bconst])
        MS("gpsimd", maskT[:], 1.0, [bconst])
        MS("gpsimd", maskT[:].rearrange("p (h t) -> p h t", t=16)[:, :, 0:1], 0.0, [bconst])
        MS("gpsimd", mask64[:], 1.0, [bconst])
        affine(mask64[0:64, :], mask64[0:64, :], [[1, 64]], ALU.is_ge, 0.0, 0, -1, [bconst], [bconst])
        affine(mask64[64:128, :], mask64[64:128, :], [[1, 64]], ALU.is_ge, 0.0, 0, -1, [bconst], [bconst])
        DMA(Dfm[:], ssm_d.rearrange("(k p) -> p k", p=128), [], [bconst], sem="c0")
        DMA(bfb[:], b_f.partition_broadcast(128), [], [bconst], sem="c0")
        DMA(knb[:], k_norm.partition_broadcast(128), [], [bconst], sem="c0")
        DMA(qnb[:], q_norm.partition_broadcast(128), [], [bconst], sem="c0")
        for t in range(NPT):
            DMAU(X[:, t, :], xp[t * 128:(t + 1) * 128, :], [], [bX[t]])
        for t in range(2):
            DMAU(X[:, NPT + t, :], xs[t * 128:(t + 1) * 128, :], [], [bX[NPT + t]])

        def phase0():
            cv = Carver()
            b0 = Buf("p0")
            araw = cv.take([128, 2, 128], F32)
            A = cv.take([128, 2, 64], F32)
            dtb = cv.take([128, 64], F32)
            Pw = cv.take([128, 9, 2, 64], F32)
            t1 = cv.take([128, 64], F32); t2 = cv.take([128, 64], F32); t3 = cv.take([128, 64], F32)
            qq = cv.take([128, 2, 64], F32)
            A8 = cv.take([128, 8, 2, 64], F32)
            for ri, src in enumerate((a_re, a_im)):
                DMA(araw[0:64, ri, 0:64], src, [], [b0], sem="c0")
                DMA(araw[0:64, ri, 64:128], src, [], [b0], sem="c0")
            DMA(dtb[:], log_dt.partition_broadcast(128), [], [b0], sem="c0")
            for ri in range(2):
                pb, bb = bank()
                TR(pb[:, 0:64], araw[0:64, ri, :], identf[0:64, 0:64], [b0, bconst], [bb])
                CP("vector", A[:, ri, :], pb[:, 0:64], [bb], [b0])
            ACT(dtb, dtb, AF.Exp, [b0], [b0])
            ar, ai = A[:, 0, :], A[:, 1, :]
            TT("vector", t1, ar, dtb, ALU.mult, [b0], [b0])
            ACT(t1, t1, AF.Exp, [b0], [b0])
            TT("vector", t2, ai, dtb, ALU.mult, [b0], [b0])

            def sin_of(dst, src, shift):
                TS("vector", t3, src, shift, None, ALU.add, None, [b0], [b0])
                CP("vector", dst, t3, [b0], [b0])
                for kk in range(1, 7):
                    thr = (2 * kk - 1) * PI
                    E("vector", lambda h, thr=thr: h.tensor_scalar(out=qq[:, 0, :], in0=t3, scalar1=thr, scalar2=-2 * PI,
                                                                   op0=ALU.is_gt, op1=ALU.mult), [b0], [b0])
                    TT("vector", dst, dst, qq[:, 0, :], ALU.add, [b0], [b0])
                ACT(dst, dst, AF.Sin, [b0], [b0])

            sin_of(Pw[:, 1, 1, :], t2, 0.0)
            sin_of(Pw[:, 1, 0, :], t2, PI / 2)
            TT("vector", Pw[:, 1, 0, :], Pw[:, 1, 0, :], t1, ALU.mult, [b0], [b0])
            TT("vector", Pw[:, 1, 1, :], Pw[:, 1, 1, :], t1, ALU.mult, [b0], [b0])
            MS("vector", Pw[:, 0, 0, :], 1.0, [b0]); MS("vector", Pw[:, 0, 1, :], 0.0, [b0])

            def cmul(dr, di, xr, xi, yr, yi, eng="vector"):
                TT(eng, t1, xr, yr, ALU.mult, [b0], [b0])
                TT(eng, t2, xi, yi, ALU.mult, [b0], [b0])
                TT(eng, t3, xr, yi, ALU.mult, [b0], [b0])
                TT(eng, dr, t1, t2, ALU.subtract, [b0], [b0])
                TT(eng, t1, xi, yr, ALU.mult, [b0], [b0])
                TT(eng, di, t3, t1, ALU.add, [b0], [b0])

            for tau in range(2, 9):
                cmul(Pw[:, tau, 0, :], Pw[:, tau, 1, :], Pw[:, tau - 1, 0, :], Pw[:, tau - 1, 1, :], Pw[:, 1, 0, :], Pw[:, 1, 1, :])
            CP("vector", A8[:, 0, :, :], Pw[:, 8, :, :], [b0], [b0])
            for k in range(1, 8):
                cmul(A8[:, k, 0, :], A8[:, k, 1, :], A8[:, k - 1, 0, :], A8[:, k - 1, 1, :], A8[:, k - 1, 0, :], A8[:, k - 1, 1, :])
            TT("vector", t1, ar, ar, ALU.mult, [b0], [b0])
            TT("vector", t2, ai, ai, ALU.mult, [b0], [b0])
            TT("vector", t1, t1, t2, ALU.add, [b0], [b0])
            E("vector", lambda h: h.reciprocal(out=t1, in_=t1), [b0], [b0])
            TS("vector", t2, Pw[:, 1, 0, :], -1.0, None, ALU.add, None, [b0], [b0])
            TT("vector", t3, t2, ar, ALU.mult, [b0], [b0])
            TT("vector", qq[:, 0, :], Pw[:, 1, 1, :], ai, ALU.mult, [b0], [b0])
            TT("vector", qq[:, 0, :], qq[:, 0, :], t3, ALU.add, [b0], [b0])
            TT("vector", t3, t2, ai, ALU.mult, [b0], [b0])
            TT("vector", qq[:, 1, :], Pw[:, 1, 1, :], ar, ALU.mult, [b0], [b0])
            TT("vector", qq[:, 1, :], qq[:, 1, :], t3, ALU.subtract, [b0], [b0])
            TT("vector", qq[:, 0, :], qq[:, 0, :], t1, ALU.mult, [b0], [b0])
            TT("vector", qq[:, 1, :], qq[:, 1, :], t1, ALU.mult, [b0], [b0])

            for ri in range(2):
                for k in range(8):
                    src = A8[:, k, ri, :].rearrange("p (a b) -> p a b", b=2)
                    CP("vector", A8s[0:64, ri, k, :], src[0:64, :, 0], [b0], [bA8s])
                    CP("vector", A8s[64:128, ri, k, :], src[64:128, :, 1], [b0], [bA8s])

            PwR = cv.take([128, 2, 64, 8], F32)
            for j in range(8):
                for ri in range(2):
                    CP("gpsimd", PwR[:, ri, :, j], Pw[:, 7 - j, ri, :], [b0], [b0])
            Braw = cv.take([128, 2, 8, 16], F32)
            Craw = cv.take([128, 2, 128], F32)
            Cc = cv.take([128, 2, 8, 16], F32)
            bbar = cv.take([128, 2, 8, 16], F32)
            tm = [cv.take([128, 1152], F32) for _ in range(4)]
            CAb = cv.take([128, 2, 9, 8, 16], F32)
            BBst = cv.take([128, 8, 16], F32)
            CAst = cv.take([128, 8, 128], F32)
            Kpad = cv.take([128, 8, 240], BF16)
            Bpw = cv.take([128, 2, 8, 128], F32)
            TB2 = [cv.take([128, 3072], BF16) for _ in range(2)]
            bBr, bCr, bCc_, bbb, bCA, bBB, bCAp, bBpw, bTB = [Buf(n) for n in "Braw Craw Cc bbar CAb BBpad CApad Bpw TB".split()]
            btm = [Buf(f"tm{i}") for i in range(4)]
            bKp = Buf("Kpad")
            MS("vector", Kpad, 0.0, [bKp])

            def cmulv(dr, di, xr, xi, yr, yi, shape, Rx, Wd, neg_im=False):
                n = int(np.prod(shape))
                names = " ".join(f"d{i}" for i in range(len(shape)))
                kw = {f"d{i}": shape[i] for i in range(len(shape))}
                tv = [t_[:, 0:n].rearrange(f"p ({names}) -> p {names}", **kw) for t_ in tm]
                TT("vector", tv[0], xr, yr, ALU.mult, Rx, [btm[0]])
                TT("gpsimd", tv[1], xi, yi, ALU.mult, Rx, [btm[1]])
                TT("vector", tv[2], xr, yi, ALU.mult, Rx, [btm[2]])
                TT("vector", tv[3], xi, yr, ALU.mult, Rx, [btm[3]])
                TT("vector", dr, tv[0], tv[1], ALU.subtract, [btm[0], btm[1]], Wd)
                if neg_im:
                    E("vector", lambda h: h.scalar_tensor_tensor(out=di, in0=tv[2], scalar=-1.0, in1=tv[3], op0=ALU.mult, op1=ALU.subtract),
                      [btm[2], btm[3]], Wd)
                else:
                    TT("vector", di, tv[2], tv[3], ALU.add, [btm[2], btm[3]], Wd)

            bTB2 = [Buf("TB0"), Buf("TB1")]
            for fc in range(8):
                gs = slice(fc * 8, fc * 8 + 8)
                TB = TB2[fc % 2]; bTB = bTB2[fc % 2]
                TBt = TB[:, 0:1024].rearrange("p (g m) -> p g m", g=8)
                TBb = TB[:, 1024:2048].rearrange("p (g r m) -> p g r m", g=8, r=2)
                TBc = TB[:, 2048:3072].rearrange("p (a r m) -> p a r m", a=4, r=2)
                for ri, src in enumerate((b_re, b_im)):
                    for half in range(2):
                        DMA(Braw[half * 64:(half + 1) * 64, ri, :, :], src[gs].rearrange("g p c -> p g c"), [], [bBr], sem="c0")
                for ri, src in enumerate((c_re, c_im)):
                    rows = src[gs].rearrange("g c p -> (g c) p")
                    DMA(Craw[:, ri, 0:64], rows, [], [bCr], sem="c0")
                    DMA(Craw[:, ri, 64:128], rows, [], [bCr], sem="c0")
                for ri in range(2):
                    pb, bb = bank()
                    TR(pb[:, 0:128], Craw[:, ri, :], identf[:], [bCr, bconst], [bb])
                    CP("scalar", Cc[:, ri, :, :], pb[:, 0:128].rearrange("p (g c) -> p g c", g=8), [bb], [bCc_])
                bq = lambda ri: qq[:, ri, gs].unsqueeze(2).to_broadcast([128, 8, 16])
                cmulv(bbar[:, 0], bbar[:, 1], Braw[:, 0], Braw[:, 1], bq(0), bq(1), [8, 16], [bBr, b0], [bbb])
                CP("scalar", BBst[0:64], bbar[0:64, 0], [bbb], [bBB])
                CP("scalar", BBst[64:128], bbar[64:128, 1], [bbb], [bBB])
                cb = lambda ri: Cc[:, ri].unsqueeze(1).to_broadcast([128, 9, 8, 16])
                pbc = lambda ri: Pw[:, :, ri, gs].unsqueeze(3).to_broadcast([128, 9, 8, 16])
                cmulv(CAb[:, 0], CAb[:, 1], cb(0), cb(1), pbc(0), pbc(1), [9, 8, 16], [bCc_, b0], [bCA], neg_im=True)
                for hf, ri in ((0, 0), (1, 1)):
                    rws = slice(64 * hf, 64 * hf + 64)
                    CP("scalar" if hf else "gpsimd", CAst[rws].rearrange("p g (t c) -> p t g c", c=16), CAb[rws, ri, 0:8], [bCA], [bCAp])
                for ri in range(2):
                    v = CAb[:, ri, 1:9].rearrange("p t (a b) c -> p t a b c", b=2)
                    CP("scalar", TBc[0:64, :, ri, :].rearrange("p a (i c) -> p i a c", c=16), v[0:64, :, :, 0, :], [bCA], [bTB])
                    CP("scalar", TBc[64:128, :, ri, :].rearrange("p a (i c) -> p i a c", c=16), v[64:128, :, :, 1, :], [bCA], [bTB])
                bb_ = lambda ri: bbar[:, ri].unsqueeze(2).to_broadcast([128, 8, 8, 16])
                pr_ = lambda ri: PwR[:, ri, gs, :].unsqueeze(3).to_broadcast([128, 8, 8, 16])
                bo_ = lambda ri: Bpw[:, ri].rearrange("p g (j c) -> p g j c", c=16)
                cmulv(bo_(0), bo_(1), bb_(0), bb_(1), pr_(0), pr_(1), [8, 8, 16], [bbb, b0], [bBpw])
                pk0, bk0 = bank()
                pk1, bk1 = bank()
                for g8 in range(8):
                    pk_, bk_ = (pk0, bk0) if g8 < 4 else (pk1, bk1)
                    MM(pk_[0:16, (g8 % 4) * 128:(g8 % 4) * 128 + 128], BBst[:, g8, :], CAst[:, g8, :], g8 % 4 == 0, True, [bBB, bCAp], [bk_],
                       skip_group_check=True)
                CP("scalar", Kpad[0:16, 0:4, 112:240], pk0[0:16, :].rearrange("p (g m) -> p g m", g=4), [bk0], [bKp])
                CP("scalar", Kpad[0:16, 4:8, 112:240], pk1[0:16, :].rearrange("p (g m) -> p g m", g=4), [bk1], [bKp])
                for g8 in range(8):
                    for ri in range(2):
                        pb, bb = bank()
                        TR(pb[:, 0:64], Bpw[0:64, ri, g8, :], identf[0:64, 0:64], [bBpw, bconst], [bb])
                        CP("scalar" if ri else "vector", TBb[:, g8, ri, :], pb[:, 0:64], [bb], [bTB])
                    pb, bb = bank()
                    for j in range(8):
                        w0 = (7 - j) * 16
                        MM(pb[:, 0:128], Wm[0:16, 0, w0:w0 + 128], Kpad[0:16, g8, w0:w0 + 128], j == 0, j == 7, [bKp, bconst], [bb])
                    CP("scalar" if g8 % 2 else "vector", TBt[:, g8, :], pb[:, 0:128], [bb], [bTB])
                DMA(tbl_d[fc], TB, [bTB], [btbl[fc], bTB], sem="c1")

        bA8s = Buf("A8s")
        btbl = [Buf(f"tbl{i}") for i in range(8)]
        phase0()
        P.barrier()
        betb = [Buf(f"etb{i}") for i in range(8)]

        def phase0b():
            cv = Carver()
            b0 = Buf("p0b")
            a8r, a8i = A8s[:, 0, 0, :], A8s[:, 1, 0, :]
            r8 = cv.take([128, 32], F32); inv = cv.take([128, 32], F32); hlf = cv.take([128, 32], F32)
            w1 = cv.take([128, 32], F32); w2 = cv.take([128, 32], F32)
            Uk = cv.take([128, 8, 2, 32], F32)
            TT("vector", w1, a8r, a8r, ALU.mult, [bA8s], [b0])
            TT("vector", w2, a8i, a8i, ALU.mult, [bA8s], [b0])
            TT("vector", w1, w1, w2, ALU.add, [b0], [b0])
            MS("gpsimd", hlf, 0.5, [b0])
            TT("gpsimd", r8, w1, hlf, ALU.pow, [b0], [b0])
            E("vector", lambda h: h.reciprocal(out=inv, in_=r8), [b0], [b0])
            TT("vector", Uk[:, 0, 0, :], a8r, inv, ALU.mult, [b0, bA8s], [b0])
            TT("vector", Uk[:, 0, 1, :], a8i, inv, ALU.mult, [b0, bA8s], [b0])
            for k in range(7):
                ur, ui = Uk[:, k, 0, :], Uk[:, k, 1, :]
                TT("vector", w1, ur, ur, ALU.mult, [b0], [b0])
                TT("vector", w2, ui, ui, ALU.mult, [b0], [b0])
                TT("vector", Uk[:, k + 1, 0, :], w1, w2, ALU.subtract, [b0], [b0])
                TT("vector", w1, ur, ui, ALU.mult, [b0], [b0])
                TT("vector", Uk[:, k + 1, 1, :], w1, w1, ALU.add, [b0], [b0])
            Eb = cv.take([128, 2, 16, NCH], F32)
            Cf = cv.take([128, 16, NCH], F32)
            T1 = cv.take([128, 16, 128], F32); T2 = cv.take([128, 16, 128], F32)
            for batch in range(2):
                ps_ = slice(16 * batch, 16 * batch + 16)
                MS("vector", Eb[:, 0, :, 0:1], 1.0, [b0]); MS("vector", Eb[:, 1, :, 0:1], 0.0, [b0])
                for k in range(8):
                    n = 1 << k
                    Ur = Uk[:, k, 0, ps_].unsqueeze(2).to_broadcast([128, 16, n])
                    Ui = Uk[:, k, 1, ps_].unsqueeze(2).to_broadcast([128, 16, n])
                    lo_r, lo_i = Eb[:, 0, :, 0:n], Eb[:, 1, :, 0:n]
                    TT("vector", T1[:, :, 0:n], lo_r, Ur, ALU.mult, [b0], [b0])
                    TT("vector", T2[:, :, 0:n], lo_i, Ui, ALU.mult, [b0], [b0])
                    TT("vector", Eb[:, 0, :, n:2 * n], T1[:, :, 0:n], T2[:, :, 0:n], ALU.subtract, [b0], [b0])
                    TT("vector", T1[:, :, 0:n], lo_i, Ur, ALU.mult, [b0], [b0])
                    TT("vector", T2[:, :, 0:n], lo_r, Ui, ALU.mult, [b0], [b0])
                    TT("vector", Eb[:, 1, :, n:2 * n], T1[:, :, 0:n], T2[:, :, 0:n], ALU.add, [b0], [b0])
                for ri in range(2):
                    CP("vector", Eb[:, ri, :, 256:NCH].rearrange("p a (s j) -> p a s j", j=8),
                       Eb[:, ri, :, 0:8].unsqueeze(2).to_broadcast([128, 16, 4, 8]), [b0], [b0])
                MS("vector", Cf, 1.0, [b0])
                MS("vector", Cf[:, :, 0:1], 0.0, [b0])
                MS("vector", Cf[:, :, 256:NCH].rearrange("p a (s j) -> p a s j", j=8)[:, :, :, 0:1], 0.0, [b0])
                TT("vector", Cf, Cf, r8[:, ps_].unsqueeze(2).to_broadcast([128, 16, NCH]), ALU.mult, [b0], [b0])
                for q in range(4):
                    fc = 4 * batch + q
                    dv = etb_d[fc].rearrange("p k (a n) -> p k a n", a=4)
                    DMA(dv[:, 0], Eb[:, 0, 4 * q:4 * q + 4, :], [b0], [betb[fc]])
                    DMA(dv[:, 1], Eb[:, 1, 4 * q:4 * q + 4, :], [b0], [betb[fc]])
                    DMA(dv[:, 2], Cf[:, 4 * q:4 * q + 4, :], [b0], [betb[fc]])

        phase0b()
        P.barrier()

        class Ring:
            def __init__(self, cv):
                self.slots = [cv.take([128, 2048], BF16) for _ in range(6)]
                self.bs = [Buf(f"slot{i}") for i in range(6)]
                self.stg = [cv.take([128, 2048], F32) for _ in range(2)]
                self.bstg = [Buf(f"stg{i}") for i in range(2)]
                self.jobs = []
                self.released = []
                self.where = []
                self.casters = []
                self.owner = [None] * 6
                self.nslot = 0
                self.nstg = 0
                self.issued = 0

            def reset_bufs(self):
                self.bs = [Buf(f"slot{i}") for i in range(6)]
                self.bstg = [Buf(f"stg{i}") for i in range(2)]

            def add(self, src, a, b, raw=False, caster="gpsimd"):
                self.jobs.append((src, a, b, raw))
                self.casters.append(caster)
                self.released.append(False)
                self.where.append(None)
                return len(self.jobs) - 1

            def pump(self, limit=None):
                while self.issued < len(self.jobs) and (limit is None or self.issued <= limit):
                    k = self.issued
                    src, a, b, raw = self.jobs[k]
                    t = self.nstg % 2
                    st = self.stg[t][:, 0:a * b].rearrange("p (a b) -> p a b", a=a)
                    if raw:
                        P.emit("sync", lambda h, st=st, src=src: h.dma_start(out=st, in_=src), [], [self.bstg[t]], dma=f"stg{t}")
                        self.where[k] = ("stg", t)
                    else:
                        s = None
                        for q in range(6):
                            c = (self.nslot + q) % 6
                            own = self.owner[c]
                            if own is None or self.released[own]:
                                s = c
                                break
                        if s is None:
                            break
                        sl = self.slots[s][:, 0:a * b].rearrange("p (a b) -> p a b", a=a)
                        P.emit("sync", lambda h, st=st, src=src: h.dma_start(out=st, in_=src), [], [self.bstg[t]], dma=f"stg{t}")
                        CP(self.casters[k], sl, st, [self.bstg[t]], [self.bs[s]])
                        self.owner[s] = k
                        self.where[k] = ("slot", s)
                        self.nslot = s + 1
                    self.nstg += 1
                    self.issued += 1

            def get(self, k):
                assert k < self.issued, (k, self.issued)
                src, a, b, raw = self.jobs[k]
                kind, idx = self.where[k]
                if kind == "stg":
                    return self.stg[idx][:, 0:a * b].rearrange("p (a b) -> p a b", a=a), self.bstg[idx]
                return self.slots[idx][:, 0:a * b].rearrange("p (a b) -> p a b", a=a), self.bs[idx]

            def release(self, k):
                self.released[k] = True

        cvF = Carver()
        ring = Ring(cvF)
        ring_end = cvF.off
        actT = [cvF.take([128, 2, 512], BF16) for _ in range(3)]
        bact = [Buf(f"act{i}") for i in range(3)]
        stmp = [cvF.take([128, 512], F32) for _ in range(2)]
        bstmp = [Buf(f"stmp{i}") for i in range(2)]
        xn = [cvF.take([128, D], BF16) for _ in range(2)]
        bxn = [Buf(f"xn{i}") for i in range(2)]
        gbc = cvF.take([128, D], F32); bgbc = Buf("gbc")
        junk = cvF.take([128, D], BF16); bjunk = Buf("junk")
        ffn_end = cvF.off
        cnt = {"act": 0, "stmp": 0, "xn": 0}

        def norm_T(gain_row):
            DMAU(gbc, gain_row.partition_broadcast(128), [], [bgbc])
            for t in range(NT):
                ACT(junk, X[:, t, :], AF.Square, [bX[t]], [bjunk, bssq], accum_out=ssq[:, t:t + 1])
            ACT(rstd[:], ssq[:], AF.Sqrt, [bssq], [brstd], scale=1.0 / D, bias=EPS)
            E("vector", lambda h: h.reciprocal(out=rstd[:], in_=rstd[:]), [brstd], [brstd])
            for t in range(NT):
                i = cnt["xn"] % 2
                cnt["xn"] += 1
                STT(xn[i], X[:, t, :], rstd[:, t:t + 1], gbc, ALU.mult, ALU.mult, [bX[t], brstd, bgbc], [bxn[i]])
                pb, bb = bank()
                pT = pb[:].bitcast(BF16)
                for kc in range(KC):
                    TR(pT[:, kc * 128:(kc + 1) * 128], xn[i][:, kc * 128:(kc + 1) * 128], ident[:], [bxn[i], bconst], [bb])
                CP("scalar", hT[:, :, t * 128:(t + 1) * 128], pT.rearrange("p (k n) -> p k n", k=KC), [bb], [bhT[t]])

        TBLK = [(0, 4), (4, 4), (8, 4), (12, 4), (16, 2)]

        def ffn(idx, gain_row):
            base = len(ring.jobs)
            for g in range(NGRP):
                c0 = g * FG
                ring.add(wg_d[idx].rearrange("(k p) n -> p k n", p=128)[:, :, c0:c0 + FG], KC, FG)
                ring.add(wu_d[idx].rearrange("(k p) n -> p k n", p=128)[:, :, c0:c0 + FG], KC, FG)
                ring.add(wd_d[idx, c0:c0 + FG, :].rearrange("(k p) n -> p k n", p=128), 2, D)
            ring.pump(limit=base + 5)
            norm_T(gain_row)
            pend = None

            def down(g, tb, ai):
                t0, ntl = TBLK[tb]
                wdv, bwd = ring.get(base + 3 * g + 2)
                for tt in range(ntl):
                    t = t0 + tt
                    for half in range(2):
                        pb, bb = bank()
                        for fc in range(2):
                            MM(pb[:, :], actT[ai][:, fc, tt * 128:(tt + 1) * 128], wdv[:, fc, half * 512:(half + 1) * 512],
                               fc == 0, fc == 1, [bact[ai], bwd], [bb])
                        xs_ = X[:, t, half * 512:(half + 1) * 512]
                        STT(xs_, pb[:, :], 0.5, xs_, ALU.mult, ALU.add, [bb, bX[t]], [bX[t]])
                if tb == len(TBLK) - 1:
                    for q in range(3):
                        ring.release(base + 3 * g + q)
                    ring.pump(limit=base + 3 * g + 8)

            for g in range(NGRP):
                wgv, bwg = ring.get(base + 3 * g)
                wuv, bwu = ring.get(base + 3 * g + 1)
                for tb, (t0, ntl) in enumerate(TBLK):
                    ntok = ntl * 128
                    ai = cnt["act"] % 3
                    cnt["act"] += 1
                    for fc in range(2):
                        pg, bg_ = bank()
                        pu, bu_ = bank()
                        rb = [bhT[t0 + q] for q in range(ntl)]
                        for kc in range(KC):
                            MM(pg[:, 0:ntok], wgv[:, kc, fc * 128:(fc + 1) * 128], hT[:, kc, t0 * 128:t0 * 128 + ntok],
                               kc == 0, kc == KC - 1, [bwg] + rb, [bg_])
                        for kc in range(KC):
                            MM(pu[:, 0:ntok], wuv[:, kc, fc * 128:(fc + 1) * 128], hT[:, kc, t0 * 128:t0 * 128 + ntok],
                               kc == 0, kc == KC - 1, [bwu] + rb, [bu_])
                        si = cnt["stmp"] % 2
                        cnt["stmp"] += 1
                        ACT(stmp[si][:, 0:ntok], pg[:, 0:ntok], AF.Silu, [bg_], [bstmp[si]])
                        TT("vector", actT[ai][:, fc, 0:ntok], stmp[si][:, 0:ntok], pu[:, 0:ntok], ALU.mult,
                           [bstmp[si], bu_], [bact[ai]])
                    if pend is not None:
                        down(*pend)
                    pend = (g, tb, ai)
            down(*pend)

        def proj_jobs(wsrc, col0, ncols, caster="gpsimd"):
            base = len(ring.jobs)
            nj = (ncols + 255) // 256
            for q in range(nj):
                w = min(256, ncols - q * 256)
                ring.add(wsrc.rearrange("(k p) n -> p k n", p=128)[:, :, col0 + q * 256:col0 + q * 256 + w], KC, w, caster=caster)
            ring.pump()
            return base

        def projT(wsrc, col0, ncols, per_tile, tiles=None, caster="gpsimd", base=None):
            nj = (ncols + 255) // 256
            if base is None:
                base = proj_jobs(wsrc, col0, ncols, caster)
            ring.pump()
            assert base + nj - 1 < ring.issued, (base, nj, ring.issued)
            tiles = list(range(NT)) if tiles is None else tiles
            for t in tiles:
                outs = []
                for q in range(nj):
                    w = min(256, ncols - q * 256)
                    wv, bw = ring.get(base + q)
                    if q % 2 == 0:
                        pb, bb = bank()
                    o = pb[:, (q % 2) * 256:(q % 2) * 256 + w]
                    for kc in range(KC):
                        MM(o, hT[:, kc, t * 128:(t + 1) * 128], wv[:, kc, :], kc == 0 and q % 2 == 0, kc == KC - 1,
                           [bhT[t], bw], [bb], skip_group_check=True)
                    outs.append((o, bb, q * 256, w))
                per_tile(t, outs)
            for q in range(nj):
                ring.release(base + q)
            ring.pump()

        ffn(0, ffn_norm[0])
        if upto <= 1:
            return finish(nc, P, X, bX, yp, ys, DMAU)

        allhT = list(bhT)

        def s5_layer():
            norm_T(mix_norm[0])
            P.barrier()
            cv = Carver()
            TBL = [cv.take([128, 3072], BF16) for _ in range(2)]
            bTBL = [Buf("TBL0"), Buf("TBL1")]
            UgB = [cv.take([128, 8, NCH], BF16) for _ in range(2)]
            bUgB = [[Buf(f"Ug{j}_{i}") for i in range(8)] for j in range(2)]
            Hs = cv.take([128, 2, 4, NCH], F32); bH = [Buf("Hr"), Buf("Hi")]
            pt = [cv.take([128, 4, NCH], F32) for _ in range(4)]; bpt = [Buf(f"pt{i}") for i in range(4)]
            Hp = cv.take([128, 2, 4, NCH], BF16); bHp = Buf("Hp")
            Ysb = cv.take([128, 8, NCH], BF16); bY = [Buf(f"Y{i}") for i in range(8)]
            vt, v2 = pt[0], pt[1]; bvt, bv2 = bpt[0], bpt[1]
            H0 = cv.take([128, 2, 32, 4], F32); bH0 = Buf("H0")
            ET = cv.take([128, 3, 4, NCH], F32); bET = Buf("ET")
            m16 = [cv.take([128, 16], F32) for _ in range(4)]; bm4 = Buf("m4")
            Hraw = cv.take([128, 128], F32); bHraw = Buf("Hraw")
            for ri, src in enumerate((h0r_d, h0i_d)):
                DMAU(Hraw, src.rearrange("s (a b) p -> (s a) (b p)", b=2), [], [bHraw])
                pb, bb = bank()
                TR(pb[:, 0:128], Hraw, identf[:], [bHraw, bconst], [bb])
                CP("vector", H0[:, ri, :, :], pb[:, 0:128].rearrange("p (s a) -> p a s", s=4), [bb], [bH0])
            ck(1.1)
            uview = lambda fc: hT[:, fc, :].rearrange("p (n i) -> p i n", i=8)
            HFs = {}

            def stA(fc):
                    tb_, btb = TBL[fc % 2], bTBL[fc % 2]
                    TBt = tb_[:, 0:1024].rearrange("p (g m) -> p g m", g=8)
                    TBb = tb_[:, 1024:2048].rearrange("p (g r m) -> p g r m", g=8, r=2)
                    TBc = tb_[:, 2048:3072].rearrange("p (a r m) -> p a r m", a=4, r=2)
                    uv = uview(fc)
                    prs = slice(4 * fc, 4 * fc + 4)
                    Ug = UgB[fc % 2]; bUg = bUgB[fc % 2]
                    DMAU(tb_, tbl_d[fc], [btbl[fc]], [btb])
                    for g8 in range(8):
                        pb, bb = bank()
                        for i in range(8):
                            MM(pb[:, 0:NCH], Wm[:, g8, 112 - 16 * i:240 - 16 * i], uv[:, i, :], i == 0, i == 7, [bconst] + allhT, [bb])
                        CP("scalar" if g8 % 2 else "vector", Ug[:, g8, :], pb[:, 0:NCH], [bb], [bUg[g8]])
                    ck(1.2)
                    for a in range(4):
                        for ri in range(2):
                            pb, bb = bank()
                            MM(pb[0:64, 0:NCH], TBb[:, 2 * a, ri, :], Ug[:, 2 * a, :], True, True, [btb, bUg[2 * a]], [bb])
                            MM(pb[64:128, 0:NCH], TBb[:, 2 * a + 1, ri, :], Ug[:, 2 * a + 1, :], True, True, [btb, bUg[2 * a + 1]], [bb])
                            CP("scalar" if ri else "vector", Hs[:, ri, a, :], pb[:, 0:NCH], [bb], [bH[ri]])
                    ck(1.3)
                    a8r = A8s[:, 0, 0, prs].unsqueeze(2).to_broadcast([128, 4, 4])
                    a8i = A8s[:, 1, 0, prs].unsqueeze(2).to_broadcast([128, 4, 4])
                    h0r_, h0i_ = H0[:, 0, prs, :], H0[:, 1, prs, :]
                    hr_ = Hs[:, 0, :, 256:NCH].rearrange("p a (s j) -> p a s j", j=8)[:, :, :, 0]
                    hi_ = Hs[:, 1, :, 256:NCH].rearrange("p a (s j) -> p a s j", j=8)[:, :, :, 0]
                    mm_ = [m16[q].rearrange("p (a s) -> p a s", a=4) for q in range(4)]
                    TT("gpsimd", mm_[0], h0r_, a8r, ALU.mult, [bH0, bA8s], [bm4])
                    TT("gpsimd", mm_[1], h0i_, a8i, ALU.mult, [bH0, bA8s], [bm4])
                    TT("gpsimd", mm_[2], h0r_, a8i, ALU.mult, [bH0, bA8s], [bm4])
                    TT("gpsimd", mm_[3], h0i_, a8r, ALU.mult, [bH0, bA8s], [bm4])
                    TT("gpsimd", hr_, hr_, mm_[0], ALU.add, [bm4, bH[0]], [bH[0]])
                    TT("gpsimd", hr_, hr_, mm_[1], ALU.subtract, [bm4, bH[0]], [bH[0]])
                    TT("gpsimd", hi_, hi_, mm_[2], ALU.add, [bm4, bH[1]], [bH[1]])
                    TT("gpsimd", hi_, hi_, mm_[3], ALU.add, [bm4, bH[1]], [bH[1]])

            def stB(fc):
                    tb_, btb = TBL[fc % 2], bTBL[fc % 2]
                    TBt = tb_[:, 0:1024].rearrange("p (g m) -> p g m", g=8)
                    TBb = tb_[:, 1024:2048].rearrange("p (g r m) -> p g r m", g=8, r=2)
                    TBc = tb_[:, 2048:3072].rearrange("p (a r m) -> p a r m", a=4, r=2)
                    uv = uview(fc)
                    prs = slice(4 * fc, 4 * fc + 4)
                    Ug = UgB[fc % 2]; bUg = bUgB[fc % 2]
                    ck(1.4)
                    DMAU(ET, etb_d[fc].rearrange("p k (a n) -> p k a n", a=4), [betb[fc]], [bET])
                    Er, Ei, Cf_ = ET[:, 0], ET[:, 1], ET[:, 2]
                    T1, T2, Wr, Wi = pt
                    fl = lambda v: v.rearrange("p a n -> p (a n)")
                    TT("vector", T1, Er, Hs[:, 0], ALU.mult, [bET, bH[0]], [bpt[0]])
                    TT("vector", T2, Ei, Hs[:, 1], ALU.mult, [bET, bH[1]], [bpt[1]])
                    TT("vector", Wr, T1, T2, ALU.add, [bpt[0], bpt[1]], [bpt[2]])
                    TT("vector", T1, Er, Hs[:, 1], ALU.mult, [bET, bH[1]], [bpt[0]])
                    TT("vector", T2, Ei, Hs[:, 0], ALU.mult, [bET, bH[0]], [bpt[1]])
                    TT("vector", Wi, T1, T2, ALU.subtract, [bpt[0], bpt[1]], [bpt[3]])
                    E("vector", lambda h: h.tensor_tensor_scan(out=fl(Hs[:, 0]), data0=fl(Cf_), data1=fl(Wr), initial=0.0,
                                                               op0=ALU.mult, op1=ALU.add), [bET, bpt[2]], [bH[0]])
                    E("vector", lambda h: h.tensor_tensor_scan(out=fl(Hs[:, 1]), data0=fl(Cf_), data1=fl(Wi), initial=0.0,
                                                               op0=ALU.mult, op1=ALU.add), [bET, bpt[3]], [bH[1]])
                    TT("vector", T1, Er, Hs[:, 0], ALU.mult, [bET, bH[0]], [bpt[0]])
                    TT("vector", T2, Ei, Hs[:, 1], ALU.mult, [bET, bH[1]], [bpt[1]])
                    TT("vector", Wr, T1, T2, ALU.subtract, [bpt[0], bpt[1]], [bpt[2]])
                    TT("vector", T1, Er, Hs[:, 1], ALU.mult, [bET, bH[1]], [bpt[0]])
                    TT("vector", T2, Ei, Hs[:, 0], ALU.mult, [bET, bH[0]], [bpt[1]])
                    TT("vector", Wi, T1, T2, ALU.add, [bpt[0], bpt[1]], [bpt[3]])
                    HF = [Wr, Wi]; bHF = [bpt[2], bpt[3]]
                    sv_ = lambda ri: HF[ri][:, :, 256:NCH].rearrange("p a (s j) -> p a s j", j=8)
                    ck(1.5)
                    for ri in range(2):
                        eng = "vector" if ri == 0 else "gpsimd"
                        CP(eng, Hp[:, ri, :, 1:256], HF[ri][:, :, 0:255], [bHF[ri]], [bHp])
                        MS(eng, Hp[:, ri, :, 0:1], 0.0, [bHp])
                        hpv = Hp[:, ri, :, 256:NCH].rearrange("p a (s j) -> p a s j", j=8)
                        CP(eng, hpv[:, :, :, 1:8], sv_(ri)[:, :, :, 0:7], [bHF[ri]], [bHp])
                        CP(eng, hpv[:, :, :, 0], H0[:, ri, prs, :], [bH0], [bHp])
                        CP(eng, Hfin[:, ri, 0, prs], HF[ri][:, :, 255], [bHF[ri]], [bHfin])
                        CP(eng, Hfin[:, ri, 1:5, prs].rearrange("p s a -> p a s"), sv_(ri)[:, :, :, 7], [bHF[ri]], [bHfin])

            def stC(fc):
                    tb_, btb = TBL[fc % 2], bTBL[fc % 2]
                    TBt = tb_[:, 0:1024].rearrange("p (g m) -> p g m", g=8)
                    TBb = tb_[:, 1024:2048].rearrange("p (g r m) -> p g r m", g=8, r=2)
                    TBc = tb_[:, 2048:3072].rearrange("p (a r m) -> p a r m", a=4, r=2)
                    uv = uview(fc)
                    prs = slice(4 * fc, 4 * fc + 4)
                    Ug = UgB[fc % 2]; bUg = bUgB[fc % 2]
                    ck(1.6)
                    for g8 in range(8):
                        a, h2 = g8 // 2, g8 % 2
                        rows = slice(64 * h2, 64 * h2 + 64)
                        pb, bb = bank()
                        MM(pb[:, 0:NCH], TBt[:, g8, :], Ug[:, g8, :], True, False, [btb, bUg[g8]], [bb])
                        MM(pb[:, 0:NCH], TBc[rows, a, 0, :], Hp[rows, 0, a, :], False, False, [btb, bHp], [bb])
                        MM(pb[:, 0:NCH], TBc[rows, a, 1, :], Hp[rows, 1, a, :], False, True, [btb, bHp], [bb])
                        CP("scalar" if g8 % 2 else "vector", Ysb[:, g8, :], pb[:, 0:NCH], [bb], [bY[g8]])
                    ck(1.7)
                    for ih in range(2):
                        for ii in range(4):
                            i = 4 * ih + ii
                            pb, bb = bank()
                            for g8 in range(8):
                                MM(pb[:, 0:NCH], Wm[:, i, 112 - 16 * g8:240 - 16 * g8], Ysb[:, g8, :], g8 == 0, g8 == 7, [bconst, bY[g8]], [bb])
                            STT(vt[:, ii, :], uv[:, i, :], Dfm[:, fc:fc + 1], pb[:, 0:NCH], ALU.mult, ALU.add, [bb, bconst] + allhT, [bvt])
                        TT("gpsimd", v2, vt, vt, ALU.mult, [bvt], [bv2])
                        TS("gpsimd", v2, v2, 0.044715, 1.0, ALU.mult, ALU.add, [bv2], [bv2])
                        TT("gpsimd", v2, v2, vt, ALU.mult, [bv2, bvt], [bv2])
                        ACT(v2, v2, AF.Sigmoid, [bv2], [bv2], scale=1.5957691216057308)
                        TT("vector", uv[:, 4 * ih:4 * ih + 4, :], vt, v2, ALU.mult, [bvt, bv2], allhT)

            stA(0)
            for fc in range(8):
                stB(fc)
                if fc + 1 < 8:
                    stA(fc + 1)
                stC(fc)
            ck(1.8)
            Hout = [cv.take([128, 128], F32)[0:32, :] for _ in range(2)]; bHout = [Buf("Hout0"), Buf("Hout1")]
            nho = 0
            for ri, (dp, ds) in enumerate(((psr, ssr), (psi, ssi))):
                for j in range(5):
                    pb, bb = bank()
                    TR(pb[0:32, 0:128], Hfin[:, ri, j, :], identf[:], [bHfin, bconst], [bb])
                    ho, bho = Hout[nho % 2], bHout[nho % 2]
                    nho += 1
                    CP("vector", ho, pb[0:32, 0:128], [bb], [bho])
                    dst = dp if j == 0 else ds[j - 1]
                    DMAU(dst.rearrange("(a b) p -> a (b p)", b=2), ho, [bho], [bho], eng="sync")
            ck(1.9)
            P.barrier()
            ring.reset_bufs()

        def glu():
            bases = []
            for half in range(2):
                bases.append(len(ring.jobs))
                for q in range(2):
                    c0 = half * 512 + q * 256
                    ring.add(w_glu_a.rearrange("(k p) n -> p k n", p=128)[:, :, c0:c0 + 256], KC, 256)
                    ring.add(w_glu_b.rearrange("(k p) n -> p k n", p=128)[:, :, c0:c0 + 256], KC, 256)
            for half in range(2):
                base = bases[half]
                ring.pump()
                for t in range(NT):
                    pa, ba = bank()
                    pb_, bb_ = bank()
                    for q in range(2):
                        wa, bwa = ring.get(base + 2 * q)
                        wb, bwb = ring.get(base + 2 * q + 1)
                        for kc in range(KC):
                            MM(pa[:, q * 256:(q + 1) * 256], hT[:, kc, t * 128:(t + 1) * 128], wa[:, kc, :], kc == 0 and q == 0,
                               kc == KC - 1, [bhT[t], bwa], [ba], skip_group_check=True)
                        for kc in range(KC):
                            MM(pb_[:, q * 256:(q + 1) * 256], hT[:, kc, t * 128:(t + 1) * 128], wb[:, kc, :], kc == 0 and q == 0,
                               kc == KC - 1, [bhT[t], bwb], [bb_], skip_group_check=True)
                    si = cnt["stmp"] % 2
                    cnt["stmp"] += 1
                    ACT(stmp[si], pb_[:, :], AF.Sigmoid, [bb_], [bstmp[si]])
                    TT("vector", stmp[si], stmp[si], pa[:, :], ALU.mult, [bstmp[si], ba], [bstmp[si]])
                    xs_ = X[:, t, half * 512:(half + 1) * 512]
                    TT("gpsimd", xs_, xs_, stmp[si], ALU.add, [bstmp[si], bX[t]], [bX[t]])
                for q in range(4):
                    ring.release(base + q)

        cvK = Carver()
        cvK.off = ffn_end
        ktmp = cvK.take([128, D], F32); bktmp = Buf("ktmp")
        kout = [cvK.take([128, D], F32) for _ in range(2)]; bkout = [Buf("kout0"), Buf("kout1")]
        kss = cvK.take([128, NH], F32); bkss = Buf("kss")
        lft = cvK.take([128, NH], F32); blft = Buf("lft")
        cnt["kout"] = 0

        def head_rmsnorm(outs, gain_b, dst, bdst, out_engine="gpsimd", ktmp=ktmp, bktmp=bktmp, kss=kss, bkss=bkss):
            for hb in range(2):
                o0, bb = outs[2 * hb][0], outs[2 * hb][1]
                o1 = outs[2 * hb + 1][0]
                for q, o in enumerate((o0, o1)):
                    c = hb * 512 + q * 256
                    ACT(ktmp[:, c:c + 256], o, AF.Square, [bb], [bktmp])
            E("vector", lambda h: h.tensor_reduce(out=kss, in_=ktmp.rearrange("p (h d) -> p h d", d=HD), axis=AX.X, op=ALU.add),
              [bktmp], [bkss])
            ACT(kss, kss, AF.Sqrt, [bkss], [bkss], scale=1.0 / HD, bias=EPS)
            E("vector", lambda h: h.reciprocal(out=kss, in_=kss), [bkss], [bkss])
            for hb in range(2):
                bb = outs[2 * hb][1]
                for q in range(2):
                    o = outs[2 * hb + q][0]
                    c = hb * 512 + q * 256
                    hh = c // HD
                    TT("vector", ktmp[:, c:c + 256].rearrange("p (h d) -> p h d", d=HD), o.rearrange("p (h d) -> p h d", d=HD),
                       kss[:, hh:hh + 4].unsqueeze(2).to_broadcast([128, 4, HD]), ALU.mult, [bb, bkss], [bktmp])
            TT(out_engine, dst.rearrange("p (h d) -> p h d", d=HD), ktmp.rearrange("p (h d) -> p h d", d=HD),
               gain_b.unsqueeze(1).to_broadcast([128, NH, HD]), ALU.mult, [bktmp, bconst], [bdst])

        def cumsum16(src3, bsrc, dst3, bdst):
            pw, bw = bank()
            MM(pw[:, 0:256], trif[:], src3.rearrange("p t h -> p (t h)"), True, True, [bsrc, bconst], [bw])
            pt_, bt_ = bank()
            MM(pt_[:, 0:256], onesf[:], src3.rearrange("p t h -> p h t"), True, True, [bsrc, bconst], [bt_])
            E("vector", lambda h: h.tensor_tensor_scan(out=scn[:], data0=maskT[:], data1=pt_[:, 0:256], initial=0.0,
                                                       op0=ALU.mult, op1=ALU.add), [bt_, bconst], [bscn])
            TT("vector", dst3, scn[:].rearrange("p (h t) -> p t h", t=16), pt_[:, 0:256].rearrange("p (h t) -> p t h", t=16),
               ALU.subtract, [bscn, bt_], [bdst])
            TT("vector", dst3, dst3, pw[:, 0:256].rearrange("p (t h) -> p t h", h=NH), ALU.add, [bw, bdst], [bdst])

        def row_dst(t, dp, ds):
            return dp[t * 128:(t + 1) * 128, :] if t < NPT else ds[(t - NPT) * 128:(t - NPT + 1) * 128, :]

        def kv_phase():
            norm_T(kv_norm)

            def k_tile(t, outs):
                i = cnt["kout"] % 2
                cnt["kout"] += 1
                head_rmsnorm(outs, knb[:, :], kout[i], bkout[i])
                DMAU(row_dst(t, pk, sk), kout[i], [bkout[i]], [bkout[i]], eng="sync")

            def v_tile(t, outs):
                i = cnt["kout"] % 2
                cnt["kout"] += 1
                for (o, bb, c, w) in outs:
                    CP("scalar" if (c // 512) % 2 else "vector", kout[i][:, c:c + w], o, [bb], [bkout[i]])
                DMAU(row_dst(t, pv, sv), kout[i], [bkout[i]], [bkout[i]], eng="sync")

            def f_tile(t, outs):
                o, bb, c, w = outs[0]
                TT("vector", lft, o, bfb[:, :], ALU.add, [bb, bconst], [blft])
                ACT(lft, lft, AF.Exp, [blft], [blft], scale=-1.0)
                ACT(lft, lft, AF.Ln, [blft], [blft], bias=1.0)
                TS("vector", LF[:, t, :], lft, -1.0, None, ALU.mult, None, [blft], [bLF])

            bF = proj_jobs(w_kvf, 2 * D, NH)
            bK = proj_jobs(w_kvf, 0, D)
            projT(w_kvf, 2 * D, NH, f_tile, base=bF)
            bV = proj_jobs(w_kvf, D, D)
            for q in range(4):
                DMAU(plf[q * 512:(q + 1) * 512, :].rearrange("(t p) h -> p t h", p=128), LF[:, 4 * q:4 * q + 4, :], [bLF], [], eng="sync")
            DMAU(slf.rearrange("(t p) h -> p t h", p=128), LF[:, NPT:NT, :], [bLF], [], eng="sync")
            cumsum16(LF[:, 0:NPT, :], bLF, CKc[:, 0:NPT, :], bCK)
            projT(w_kvf, 0, D, k_tile, base=bK)
            projT(w_kvf, D, D, v_tile, base=bV)

        def attention_layer():
            norm_T(mix_norm[1])
            P.barrier()
            cvA = Carver()
            cvA.off = ring_end
            QT = cvA.take([128, 8, 512], BF16); bQT = Buf("QT")
            qn = cvA.take([128, D], BF16); bqn = Buf("qn")
            KT = [cvA.take([128, 2048], BF16) for _ in range(2)]; bKT = [Buf("KT0"), Buf("KT1")]
            VA = [cvA.take([128, 16, 2, 65], BF16) for _ in range(2)]; bVA = [Buf("VA0"), Buf("VA1")]
            PT = [cvA.take([128, 512], BF16) for _ in range(5)]; bPT = [Buf(f"PT{i}") for i in range(5)]
            biasb = cvA.take([128, 17, NH], F32); bbias = Buf("bias")
            biasb2 = cvA.take([128, 17, NH], F32); bbias2 = Buf("bias2")
            c0s = cvA.take([128, NH], F32); bc0 = Buf("c0")
            osb = [cvA.take([128, 4, 128], BF16) for _ in range(2)]; bosb = [Buf("osb0"), Buf("osb1")]
            rc = cvA.take([128, 8], F32); brc = Buf("rc")
            CL = cvA.take([128, 16, NH], F32); bCL = Buf("CL")
            qtmp = cvA.take([128, D], F32); bqtmp = Buf("qtmp")
            qss = cvA.take([128, NH], F32); bqss = Buf("qss")
            KTn = QT[:, :, 256:384]
            VAn = qn.rearrange("p (h d) -> p h d", d=HD)
            bKTn = bQT; bVAn = bqn
            for i in range(2):
                MS("gpsimd", VA[i][:, :, :, 64:65], 1.0, [bVA[i]])
            onescol = VA[0][:, 0, 0, 64:65]
            st = {"pt": 0, "kv": 0, "o": 0}
            pool["ids"] = [4, 5]
            wq_v = w_q
            wo_v = w_o

            def q_proj(tiles):
                def q_tile(t, outs):
                    head_rmsnorm(outs, qnb[:, :], qn, bqn, out_engine="vector", ktmp=qtmp, bktmp=bqtmp, kss=qss, bkss=bqss)
                    pb, bb = bank()
                    pT = pb[:].bitcast(BF16)
                    for pr in range(8):
                        TR(pT[:, pr * 128:(pr + 1) * 128], qn[:, pr * 128:(pr + 1) * 128], ident[:], [bqn, bconst], [bb])
                    tt = tiles.index(t)
                    CP("scalar", QT[:, :, tt * 128:(tt + 1) * 128], pT.rearrange("p (k n) -> p k n", k=8), [bb], [bQT])
                pool["ids"] = list(range(6))
                projT(wq_v, 0, D, q_tile, tiles=tiles, caster="vector")
                pool["ids"] = [4, 5]

            def o_proj(tiles, base=None):
                def o_tile(t, outs):
                    for (o, bb, c, w) in outs:
                        xs_ = X[:, t, c:c + w]
                        TT("vector", xs_, xs_, o, ALU.add, [bb, bX[t]], [bX[t]])
                pool["ids"] = list(range(6))
                projT(wo_v, 0, D, o_tile, tiles=tiles, caster="vector", base=base)
                pool["ids"] = [4, 5]

            def load_kv(ksrc, vsrc, nkt, pair):
                i = st["kv"] % 2
                st["kv"] += 1
                cs = slice(pair * 128, (pair + 1) * 128)
                jk = ring.add(ksrc.rearrange("(t p) c -> p t c", p=128)[:, 0:nkt, cs], nkt, 128, caster="vector")
                jv = ring.add(vsrc.rearrange("(t p) c -> p t c", p=128)[:, 0:nkt, cs], nkt, 128, raw=True)
                ring.pump()
                kv_, bk_ = ring.get(jk)
                for g0 in range(0, nkt, 8):
                    n = min(8, nkt - g0)
                    pb, bb = bank()
                    pT = pb[:].bitcast(BF16)
                    for q in range(n):
                        TR(pT[:, q * 128:(q + 1) * 128], kv_[:, g0 + q, :], ident[:], [bk_, bconst], [bb])
                    CP("vector", KT[i][:, g0 * 128:(g0 + n) * 128], pT[:, 0:n * 128], [bb], [bKT[i]])
                ring.release(jk)
                vv_, bv_ = ring.get(jv)
                CP("vector", VA[i][:, 0:nkt, :, 0:64], vv_.rearrange("p t (h d) -> p t h d", h=2), [bv_], [bVA[i]])
                ring.release(jv)
                return i

            for b in range(4):
                tiles = [4 * b + q for q in range(4)]
                nkt = 4 * b + 4
                kvb = {}

                def ensure_kv(pair, nkt=nkt):
                    if pair < 8 and pair not in kvb:
                        kvb[pair] = load_kv(pk, pv, nkt, pair)

                ensure_kv(0)
                ensure_kv(1)
                q_proj(tiles)
                bo_w = proj_jobs(wo_v, 0, D, caster="vector")
                pb, bb = bank()
                MM(pb[:, 0:NH], sel64[:], CKc[:, 4 * b + 2, :], True, True, [bCK, bconst], [bb])
                CP("vector", c0s, pb[:, 0:NH], [bb], [bc0])
                TT("vector", biasb[:, 0:nkt, :], c0s.unsqueeze(1).to_broadcast([128, nkt, NH]), CKc[:, 0:nkt, :], ALU.subtract,
                   [bc0, bCK], [bbias])
                tasks = [dict(pair=pair, h2=h2, kt=kt) for pair in range(8) for kt in range(nkt) for h2 in range(2)]
                hs = {0: {}, 1: {}, "oi": 0}

                def s1(T, b=b):
                    pair, h2, kt = T["pair"], T["h2"], T["kt"]
                    i = kvb[pair]
                    rows = slice(64 * h2, 64 * h2 + 64)
                    r = kt - 4 * b
                    q0 = max(r, 0) * 128
                    ps_s, bs_ = sbank()
                    MM(ps_s[:, q0:512], KT[i][rows, kt * 128:(kt + 1) * 128], QT[rows, pair, q0:512], True, True,
                       [bKT[i], bQT], [bs_])
                    T.update(ps=ps_s, bs=bs_, q0=q0, r=r, i=i)

                def s2(T):
                    pair, h2, kt = T["pair"], T["h2"], T["kt"]
                    head = 2 * pair + h2
                    q0, r = T["q0"], T["r"]
                    pi = st["pt"] % 5
                    st["pt"] += 1
                    T["pi"] = pi
                    ACT(PT[pi][:, q0:512], T["ps"][:, q0:512], AF.Exp, [T["bs"], bbias], [bPT[pi]], scale=SCALE,
                        bias=biasb[:, kt, head:head + 1])
                    if r >= 0:
                        TT("gpsimd", PT[pi][:, q0:q0 + 128], PT[pi][:, q0:q0 + 128], trib[:], ALU.mult, [bPT[pi], bconst], [bPT[pi]])

                def s3(T, b=b, nkt=nkt, tiles=tiles):
                    pair, h2, kt = T["pair"], T["h2"], T["kt"]
                    i, pi, r = T["i"], T["pi"], T["r"]
                    if kt == 0:
                        hs[h2]["pbo"], hs[h2]["bbo"] = acc_bank()
                        if h2 == 0:
                            hs["oi"] = st["o"] % 2
                            st["o"] += 1
                    pbo, bbo, oi = hs[h2]["pbo"], hs[h2]["bbo"], hs["oi"]
                    for qt in range(max(r, 0), 4):
                        MM(pbo[:, qt * 65:(qt + 1) * 65], PT[pi][:, qt * 128:(qt + 1) * 128], VA[i][:, kt, h2, :],
                           kt == 0 and qt == 0, kt == 4 * b + qt, [bPT[pi], bVA[i]], [bbo], skip_group_check=True)
                    if kt == nkt - 1:
                        ov = pbo[:, 0:260].rearrange("p (q e) -> p q e", e=65)
                        E("vector", lambda h, ov=ov: h.reciprocal(out=rc[:, 0:4], in_=ov[:, :, 64]), [bbo], [brc])
                        TT("vector", osb[oi][:, :, 64 * h2:64 * h2 + 64], ov[:, :, 0:64],
                           rc[:, 0:4].unsqueeze(2).to_broadcast([128, 4, 64]), ALU.mult, [bbo, brc], [bosb[oi]])
                        if h2 == 1:
                            pb, bb = bank()
                            pT = pb[:].bitcast(BF16)
                            for qt in range(4):
                                TR(pT[:, qt * 128:(qt + 1) * 128], osb[oi][:, qt, :], ident[:], [bosb[oi], bconst], [bb])
                            CP("scalar", hT[:, pair, 512 * b:512 * b + 512], pT[:, 0:512], [bb], [bhT[q] for q in tiles])
                            ensure_kv(pair + 2)

                units = [tasks[2 * u:2 * u + 2] for u in range(len(tasks) // 2)]
                LA = 1
                for n in range(len(units) + LA):
                    if n < len(units):
                        for T in units[n]:
                            s1(T)
                    if n - LA >= 0:
                        for T in units[n - LA]:
                            s2(T)
                        for T in units[n - LA]:
                            s3(T)
                o_proj(tiles, base=bo_w)
            ck(5.5)

            q_proj([NPT, NPT + 1])
            for j in range(2):
                t = NPT + j
                jk = ring.add(sk[j * 128:(j + 1) * 128, :].rearrange("p (a c) -> p a c", a=8), 8, 128)
                jv = ring.add(sv[j * 128:(j + 1) * 128, :].rearrange("p (a c) -> p a c", a=8), 8, 128)
                ring.pump()
                kn_, bkn = ring.get(jk)
                pb, bb = bank()
                pT = pb[:].bitcast(BF16)
                for pr in range(8):
                    TR(pT[:, pr * 128:(pr + 1) * 128], kn_[:, pr, :], ident[:], [bkn, bconst], [bb])
                CP("scalar", KTn, pT.rearrange("p (k n) -> p k n", k=8), [bb], [bKTn])
                ring.release(jk)
                vn_, bvn = ring.get(jv)
                CP("gpsimd", qn, vn_.rearrange("p a c -> p (a c)"), [bvn], [bVAn])
                ring.release(jv)
                def prologue(s_):
                    j_, half_ = divmod(s_, 2)
                    t_ = NPT + j_
                    rt_ = slice(64 * half_, 64 * half_ + 64)
                    bz, bbz = (biasb, bbias) if s_ % 2 == 0 else (biasb2, bbias2)
                    for q in range(4):
                        DMAU(CL[:, 4 * q:4 * q + 4, :], clf_d[s_, q * 512:(q + 1) * 512, :].rearrange("(t p) h -> p t h", p=128), [], [bCL])
                    cumsum16(CL[:, :, :], bCL, CL[:, :, :], bCL)
                    pb2, bb2 = bank()
                    MM(pb2[:, 0:NH], tri2f[:], LF[:, t_, :], True, True, [bLF, bconst], [bb2])
                    TT("vector", CKc[rt_, t_, :], pb2[rt_, 0:NH], scn[rt_, :].rearrange("p (h t) -> p h t", t=16)[:, :, 15], ALU.add,
                       [bb2, bscn], [bCK])
                    pb3, bb3 = bank()
                    MM(pb3[:, 0:NH], (sel32 if half_ == 0 else sel96)[:], CKc[:, t_, :], True, True, [bCK, bconst], [bb3])
                    CP("vector", c0s, pb3[:, 0:NH], [bb3], [bc0])
                    TT("vector", bz[:, 0:16, :], c0s.unsqueeze(1).to_broadcast([128, 16, NH]), CL[:, :, :], ALU.subtract,
                       [bc0, bCL], [bbz])
                    TT("vector", bz[:, 16, :], c0s, CKc[:, t_, :], ALU.subtract, [bc0, bCK], [bbz])

                if j == 0:
                    prologue(0)
                for half in range(2):
                    s_ = 2 * j + half
                    rt = slice(64 * half, 64 * half + 64)
                    biasb_s, bbias_s = (biasb, bbias) if s_ % 2 == 0 else (biasb2, bbias2)
                    qc = slice(s_ * 64, s_ * 64 + 64)
                    kvb = {}

                    def ensure_kv(pair, s_=s_):
                        if pair < 8 and pair not in kvb:
                            kvb[pair] = load_kv(ck_d[s_], cv_d[s_], 16, pair)

                    ensure_kv(0)
                    ensure_kv(1)
                    tasks = [dict(pair=pair, h2=h2, kt=kt) for pair in range(8) for kt in range(17) for h2 in range(2)]
                    hs = {0: {}, 1: {}, "oi": 0}

                    def s1(T, qc=qc, rt=rt, half=half):
                        pair, h2, kt = T["pair"], T["h2"], T["kt"]
                        i = kvb[pair]
                        rows = slice(64 * h2, 64 * h2 + 64)
                        ps_s, bs_ = sbank()
                        if kt < 16:
                            MM(ps_s[:, 0:64], KT[i][rows, kt * 128:(kt + 1) * 128], QT[rows, pair, qc], True, True, [bKT[i], bQT], [bs_])
                        else:
                            MM(ps_s[rt, 0:64], KTn[rows, pair, 64 * half:64 * half + 64], QT[rows, pair, qc], True, True, [bKTn, bQT], [bs_])
                        T.update(ps=ps_s, bs=bs_, i=i)

                    def s2(T, rt=rt, biasb_s=biasb_s, bbias_s=bbias_s):
                        pair, h2, kt = T["pair"], T["h2"], T["kt"]
                        head = 2 * pair + h2
                        pi = st["pt"] % 5
                        st["pt"] += 1
                        T["pi"] = pi
                        if kt < 16:
                            ACT(PT[pi][:, 0:64], T["ps"][:, 0:64], AF.Exp, [T["bs"], bbias_s], [bPT[pi]], scale=SCALE,
                                bias=biasb_s[:, kt, head:head + 1])
                        else:
                            ACT(PT[pi][rt, 0:64], T["ps"][rt, 0:64], AF.Exp, [T["bs"], bbias_s], [bPT[pi]], scale=SCALE,
                                bias=biasb_s[rt, 16, head:head + 1])
                            TT("gpsimd", PT[pi][rt, 0:64], PT[pi][rt, 0:64], mask64[rt, :], ALU.mult, [bPT[pi], bconst], [bPT[pi]])

                    def s3(T, rt=rt, s_=s_, t=t):
                        pair, h2, kt = T["pair"], T["h2"], T["kt"]
                        head = 2 * pair + h2
                        i, pi = T["i"], T["pi"]
                        if kt == 0:
                            hs[h2]["pbo"], hs[h2]["bbo"] = acc_bank()
                            if h2 == 0:
                                hs["oi"] = st["o"] % 2
                                st["o"] += 1
                        pbo, bbo, oi = hs[h2]["pbo"], hs[h2]["bbo"], hs["oi"]
                        if kt < 16:
                            MM(pbo[0:64, 0:65], PT[pi][:, 0:64], VA[i][:, kt, h2, :], kt == 0, False, [bPT[pi], bVA[i]], [bbo],
                               skip_group_check=True)
                            return
                        MM(pbo[0:64, 0:64], PT[pi][rt, 0:64], VAn[rt, head, :], False, False, [bPT[pi], bVAn], [bbo], skip_group_check=True)
                        MM(pbo[0:64, 64:65], PT[pi][rt, 0:64], onescol[rt, :], False, True, [bPT[pi], bVA[0]], [bbo], skip_group_check=True)
                        E("vector", lambda h, pbo=pbo: h.reciprocal(out=rc[0:64, 0:1], in_=pbo[0:64, 64:65]), [bbo], [brc])
                        TS("vector", osb[oi][0:64, 0, 64 * h2:64 * h2 + 64], pbo[0:64, 0:64], rc[0:64, 0:1], None, ALU.mult, None,
                           [bbo, brc], [bosb[oi]])
                        if h2 == 1:
                            pb, bb = bank()
                            pT = pb[:].bitcast(BF16)
                            TR(pT[:, 0:64], osb[oi][0:64, 0, :], ident[0:64, 0:64], [bosb[oi], bconst], [bb])
                            CP("scalar", hT[:, pair, 2048 + 64 * s_:2048 + 64 * s_ + 64], pT[:, 0:64], [bb], [bhT[t]])
                            ensure_kv(pair + 2)

                    units = [tasks[2 * u:2 * u + 2] for u in range(len(tasks) // 2)]
                    LA = 1
                    for n in range(len(units) + LA):
                        if n == len(units) // 2 and s_ + 1 < 4:
                            prologue(s_ + 1)
                        if n < len(units):
                            for T in units[n]:
                                s1(T)
                        if n - LA >= 0:
                            for T in units[n - LA]:
                                s2(T)
                            for T in units[n - LA]:
                                s3(T)
            o_proj([NPT, NPT + 1])
            pool["ids"] = list(range(8))
            P.barrier()
            ring.reset_bufs()

        def ck(level):
            if upto <= level:
                raise _Stop()

        try:
            s5_layer()
            glu()
            ck(2)
            ffn(1, ffn_norm[1])
            ck(3)
            kv_phase()
            ck(4)
            ffn(2, ffn_norm[2])
            ck(5)
            attention_layer()
            ck(6)
            ffn(3, ffn_norm[3])
            raise _Stop()
        except _Stop:
            return finish(nc, P, X, bX, yp, ys, DMAU)

    return nc


def finish(nc, P, X, bX, yp, ys, DMAU):
    for t in range(NPT):
        DMAU(yp[t * 128:(t + 1) * 128, :], X[:, t, :], [bX[t]], [], eng="sync")
    for t in range(2):
        DMAU(ys[t * 128:(t + 1) * 128, :], X[:, NPT + t, :], [bX[NPT + t]], [], eng="sync")
    P.finalize()
    return nc


_NC_CACHE = {}


def _in_maps(inp, cores):
    f = lambda a: np.ascontiguousarray(np.asarray(a, dtype=np.float32))
    shared = {
        "ffn_norm": f(inp["ffn_norm"]).reshape(4, D),
        "wg": f(inp["w_ffn_gate"]).reshape(4, D, DFF),
        "wu": f(inp["w_ffn_up"]).reshape(4, D, DFF),
        "wd": f(inp["w_ffn_down"]).reshape(4, DFF, D),
        "mix_norm": f(inp["mix_norm"]),
        "a_re": f(inp["ssm_a_re"])[0], "a_im": f(inp["ssm_a_im"])[0], "log_dt": f(inp["ssm_log_dt"])[0],
        "b_re": f(inp["ssm_b_re"])[0], "b_im": f(inp["ssm_b_im"])[0],
        "c_re": f(inp["ssm_c_re"])[0], "c_im": f(inp["ssm_c_im"])[0],
        "ssm_d": f(inp["ssm_d"])[0],
        "w_glu_a": f(inp["w_glu_a"])[0], "w_glu_b": f(inp["w_glu_b"])[0],
        "kv_norm": f(inp["kv_norm"]), "w_kvf": f(inp["w_kvf"]), "b_f": f(inp["b_f"]), "k_norm": f(inp["k_norm"]),
        "w_q": f(inp["w_q"])[0], "q_norm": f(inp["q_norm"])[0], "w_o": f(inp["w_o"])[0],
    }
    xp = f(inp["x_prompt"]); xs = f(inp["x_sample"])
    ck = f(inp["cache_k"]); cv = f(inp["cache_v"]); clf = f(inp["cache_logf"])
    h0r = f(inp["state_ssm_re"]); h0i = f(inp["state_ssm_im"])
    maps = []
    for c in cores:
        m = dict(shared)
        m["xp"] = xp[c]
        m["xs"] = xs[4 * c:4 * c + 4].reshape(256, D)
        m["ck"] = ck[4 * c:4 * c + 4].reshape(4, S, D)
        m["cv"] = cv[4 * c:4 * c + 4].reshape(4, S, D)
        m["clf"] = clf[4 * c:4 * c + 4]
        m["h0r"] = h0r[4 * c:4 * c + 4, 0]
        m["h0i"] = h0i[4 * c:4 * c + 4, 0]
        maps.append(m)
    return maps


def kernel(**inp):
    if "nc" not in _NC_CACHE:
        _NC_CACHE["nc"] = build_program()
    nc = _NC_CACHE["nc"]
    cores = list(range(8))
    res = run_bass_kernel_spmd(nc, _in_maps(inp, cores), core_ids=cores).results
    cat = lambda k, shp: np.stack([np.asarray(r[k], dtype=np.float32) for r in res]).reshape(shp)
    y_p = cat("yp", (8, S, D))
    y_s = cat("ys", (32, 64, D))
    p_sr = cat("psr", (8, 1, 64, 64)); p_si = cat("psi", (8, 1, 64, 64))
    p_k = cat("pk", (8, S, NH, HD)); p_v = cat("pv", (8, S, NH, HD)); p_lf = cat("plf", (8, S, NH))
    s_sr = cat("ssr", (32, 1, 64, 64)); s_si = cat("ssi", (32, 1, 64, 64))
    s_k = cat("sk", (32, 64, NH, HD)); s_v = cat("sv", (32, 64, NH, HD)); s_lf = cat("slf", (32, 64, NH))
    return (y_p, y_s, p_sr, p_si, p_k, p_v, p_lf, s_sr, s_si, s_k, s_v, s_lf)
```
